# Optimizing a Trainium2 kernel written in Bass

```python
import jax, jax.numpy as jnp
from jax import lax
import numpy as np

D_MODEL = 1024
BATCH = 32
SEQ = 256
DEPTH = 4
DEC_BATCH = 4
DEC_SEQ = 4096
PAST_LEN = 512

GRID_W = 64
N_AB = (DEPTH + 1) // 2
N_C = DEPTH // 2
MLA_HEADS = 8
QK_NOPE_DIM = 64
QK_ROPE_DIM = 32
V_HEAD_DIM = 64
Q_LORA_RANK = 384
KV_LORA_RANK = 256
ROPE_BASE = 10000.0
ROPE_AXIS_DIM = QK_ROPE_DIM // 2
Q_BLOCK = 128
ATTN_SCALE = (QK_NOPE_DIM + QK_ROPE_DIM) ** -0.5
GMLP_GROUPS = 8
GMLP_CHUNK = 128
GMLP_WIDTH = D_MODEL // 2
GMLP_GROUP_DIM = GMLP_WIDTH // GMLP_GROUPS
QA_END = Q_LORA_RANK
KVA_END = QA_END + KV_LORA_RANK
KPE_END = KVA_END + QK_ROPE_DIM
IN_AB = KPE_END + 2 * GMLP_WIDTH
MIX_AB = MLA_HEADS * V_HEAD_DIM + GMLP_WIDTH
POOL_WINDOWS = (2, 4, 8, 16)
POOL_GROUP_DIM = D_MODEL // len(POOL_WINDOWS)
D_FF = 4 * D_MODEL
N_MOD = 6
EPS = 1e-6

kernel_name = 'hybrid_mla_gmlp_pool_diffusion_step'


def rms_norm(x, g):
    x32 = x.astype(jnp.float32)
    y = x32 * lax.rsqrt(jnp.mean(x32 * x32, axis=-1, keepdims=True) + EPS)
    return (y * g.astype(jnp.float32)).astype(x.dtype)


def axial_angles(n_tokens):
    rows = n_tokens // GRID_W
    row = jnp.repeat(jnp.arange(rows), GRID_W).astype(jnp.float32)
    col = jnp.tile(jnp.arange(GRID_W), rows).astype(jnp.float32)
    inv = ROPE_BASE ** (-jnp.arange(0, ROPE_AXIS_DIM, 2, dtype=jnp.float32) / ROPE_AXIS_DIM)
    return jnp.stack([row[:, None] * inv, col[:, None] * inv], axis=1)


def apply_axial_rope(x, ang):
    B, T, H, _ = x.shape
    F = QK_ROPE_DIM // 4
    xr = x.reshape(B, T, H, 2, 2, F)
    x1, x2 = xr[..., 0, :], xr[..., 1, :]
    cos = jnp.cos(ang)[:, None].astype(x.dtype)
    sin = jnp.sin(ang)[:, None].astype(x.dtype)
    out = jnp.stack([x1 * cos - x2 * sin, x2 * cos + x1 * sin], axis=-2)
    return out.reshape(x.shape)


def mla_attention(q_nope, q_pe, k_nope, k_pe, v):
    B, T, H, _ = q_nope.shape
    nb = T // Q_BLOCK
    qn = q_nope.reshape(B, nb, Q_BLOCK, H, QK_NOPE_DIM).swapaxes(0, 1)
    qp = q_pe.reshape(B, nb, Q_BLOCK, H, QK_ROPE_DIM).swapaxes(0, 1)

    def block(args):
        qn_b, qp_b = args
        s = (jnp.einsum('bqhd,bkhd->bhqk', qn_b, k_nope)
             + jnp.einsum('bqhd,bkd->bhqk', qp_b, k_pe))
        p = jax.nn.softmax(s.astype(jnp.float32) * ATTN_SCALE, axis=-1).astype(v.dtype)
        return jnp.einsum('bhqk,bkhd->bqhd', p, v)

    o = lax.map(block, (qn, qp))
    return o.swapaxes(0, 1).reshape(B, T, H * V_HEAD_DIM)


def chunk_gmlp(u, v, v_g, w_s, b_s):
    B, T, _ = v.shape
    vn = rms_norm(v, v_g).reshape(B, T // GMLP_CHUNK, GMLP_CHUNK, GMLP_GROUPS, GMLP_GROUP_DIM)
    s = jnp.einsum('gpq,bnqgc->bnpgc', w_s, vn) + b_s.T[:, :, None]
    return u * s.reshape(B, T, GMLP_WIDTH)


def mixer_ab(h, w_in, q_a_g, kv_a_g, w_q_b, w_kv_b, v_g, w_s, b_s, w_out, ang, ctx_ckv, ctx_kpe):
    B, T, _ = h.shape
    z = h @ w_in
    q_a, kv_a, k_pe = z[..., :QA_END], z[..., QA_END:KVA_END], z[..., KVA_END:KPE_END]
    uv = jax.nn.gelu(z[..., KPE_END:])
    q = (rms_norm(q_a, q_a_g) @ w_q_b).reshape(B, T, MLA_HEADS, QK_NOPE_DIM + QK_ROPE_DIM)
    q_nope, q_pe = q[..., :QK_NOPE_DIM], q[..., QK_NOPE_DIM:]
    c_kv = rms_norm(kv_a, kv_a_g)
    if ang is None:
        keys_ckv, keys_kpe = c_kv, k_pe
    else:
        q_pe = apply_axial_rope(q_pe, ang)
        k_pe_rot = apply_axial_rope(k_pe[:, :, None, :], ang)[:, :, 0, :]
        keys_ckv = jnp.concatenate([ctx_ckv, c_kv], axis=1)
        keys_kpe = jnp.concatenate([ctx_kpe, k_pe_rot], axis=1)
    L = keys_ckv.shape[1]
    kv = (keys_ckv @ w_kv_b).reshape(B, L, MLA_HEADS, QK_NOPE_DIM + V_HEAD_DIM)
    k_nope, v = kv[..., :QK_NOPE_DIM], kv[..., QK_NOPE_DIM:]
    attn = mla_attention(q_nope, q_pe, k_nope, keys_kpe, v)
    gm = chunk_gmlp(uv[..., :GMLP_WIDTH], uv[..., GMLP_WIDTH:], v_g, w_s, b_s)
    out = jnp.concatenate([attn, gm], axis=-1) @ w_out
    return out, c_kv, k_pe


def pool_mixer(h, w_pool, scale):
    B, T, _ = h.shape
    h32 = h.astype(jnp.float32)
    cs = jnp.concatenate([jnp.zeros_like(h32[:, :1]), jnp.cumsum(h32, axis=1)], axis=1)
    t = jnp.arange(T)
    outs = []
    for gi, w in enumerate(POOL_WINDOWS):
        sl = slice(gi * POOL_GROUP_DIM, (gi + 1) * POOL_GROUP_DIM)
        start = jnp.clip(t - w // 2, 0, T)
        end = jnp.clip(t + w // 2, 0, T)
        cnt = (end - start).astype(jnp.float32)[None, :, None]
        mean = (cs[:, end, sl] - cs[:, start, sl]) / cnt
        outs.append((mean - h32[..., sl]).astype(h.dtype) @ w_pool[gi])
    return jnp.concatenate(outs, axis=-1) * scale


def squared_relu_mlp(h, w1, w2):
    a = jax.nn.relu(h @ w1)
    return (a * a) @ w2


def trunk(x, cond, ang, ctx_ckv, ctx_kpe, w_mod, b_mod, norm1_g, norm2_g, w_in_ab, q_a_g, kv_a_g,
          w_q_b, w_kv_b, gmlp_v_g, w_spatial, b_spatial, w_out_ab, w_pool, pool_scale,
          w_ff1, w_ff2, final_g):
    ckv_list, kpe_list = [], []
    for l in range(DEPTH):
        mod = (jax.nn.silu(cond) @ w_mod[l] + b_mod[l]).reshape(-1, 1, N_MOD * D_MODEL)
        sh1, sc1, g1, sh2, sc2, g2 = jnp.split(mod, N_MOD, axis=-1)
        h = rms_norm(x, norm1_g[l]) * (1 + sc1) + sh1
        if l % 2 == 0:
            i = l // 2
            c_ckv = None if ang is None else ctx_ckv[:, i]
            c_kpe = None if ang is None else ctx_kpe[:, i]
            mix, ckv, kpe = mixer_ab(h, w_in_ab[i], q_a_g[i], kv_a_g[i], w_q_b[i], w_kv_b[i],
                                     gmlp_v_g[i], w_spatial[i], b_spatial[i], w_out_ab[i],
                                     ang, c_ckv, c_kpe)
            if ang is None:
                ckv_list.append(ckv)
                kpe_list.append(kpe)
        else:
            mix = pool_mixer(h, w_pool[l // 2], pool_scale[l // 2])
        x = x + g1 * mix
        h = rms_norm(x, norm2_g[l]) * (1 + sc2) + sh2
        x = x + g2 * squared_relu_mlp(h, w_ff1[l], w_ff2[l])
    return rms_norm(x, final_g), ckv_list, kpe_list


def setup_inputs(seed: int = 0) -> dict:
    key = jax.random.key(seed)
    ks = jax.random.split(key, 26)
    f32 = jnp.float32

    def nrm(k, shape, scale):
        return jax.random.normal(k, shape, f32) * scale

    def gain(k, shape):
        return 1.0 + 0.02 * jax.random.normal(k, shape, f32)

    D = D_MODEL
    return {
        'x_prompt': nrm(ks[0], (BATCH, SEQ, D), 1.0),
        'x_sample': nrm(ks[1], (DEC_BATCH, DEC_SEQ, D), 1.0),
        'cache_ckv': nrm(ks[2], (DEC_BATCH, N_AB, PAST_LEN, KV_LORA_RANK), 1.0),
        'cache_kpe': nrm(ks[3], (DEC_BATCH, N_AB, PAST_LEN, QK_ROPE_DIM), 1.0),
        'c': nrm(ks[4], (DEC_BATCH, D), 1.0),
        'c_ctx': nrm(ks[5], (D,), 1.0),
        'w_mod': nrm(ks[6], (DEPTH, D, N_MOD * D), 0.5 * D ** -0.5),
        'b_mod': nrm(ks[7], (DEPTH, N_MOD * D), 0.02),
        'norm1_g': gain(ks[8], (DEPTH, D)),
        'norm2_g': gain(ks[9], (DEPTH, D)),
        'w_in_ab': nrm(ks[10], (N_AB, D, IN_AB), D ** -0.5),
        'q_a_g': gain(ks[11], (N_AB, Q_LORA_RANK)),
        'kv_a_g': gain(ks[12], (N_AB, KV_LORA_RANK)),
        'w_q_b': nrm(ks[13], (N_AB, Q_LORA_RANK, MLA_HEADS * (QK_NOPE_DIM + QK_ROPE_DIM)), Q_LORA_RANK ** -0.5),
        'w_kv_b': nrm(ks[14], (N_AB, KV_LORA_RANK, MLA_HEADS * (QK_NOPE_DIM + V_HEAD_DIM)), KV_LORA_RANK ** -0.5),
        'gmlp_v_g': gain(ks[15], (N_AB, GMLP_WIDTH)),
        'w_spatial': nrm(ks[16], (N_AB, GMLP_GROUPS, GMLP_CHUNK, GMLP_CHUNK), GMLP_CHUNK ** -0.5),
        'b_spatial': gain(ks[17], (N_AB, GMLP_GROUPS, GMLP_CHUNK)),
        'w_out_ab': nrm(ks[18], (N_AB, MIX_AB, D), MIX_AB ** -0.5),
        'w_pool': nrm(ks[19], (N_C, len(POOL_WINDOWS), POOL_GROUP_DIM, POOL_GROUP_DIM), POOL_GROUP_DIM ** -0.5),
        'pool_scale': gain(ks[20], (N_C, D)),
        'w_ff1': nrm(ks[21], (DEPTH, D, D_FF), D ** -0.5),
        'w_ff2': nrm(ks[22], (DEPTH, D_FF, D), D_FF ** -0.5),
        'final_g': gain(ks[23], (D,)),
    }


def reference(x_prompt, x_sample, cache_ckv, cache_kpe, c, c_ctx, w_mod, b_mod, norm1_g, norm2_g,
              w_in_ab, q_a_g, kv_a_g, w_q_b, w_kv_b, gmlp_v_g, w_spatial, b_spatial, w_out_ab,
              w_pool, pool_scale, w_ff1, w_ff2, final_g):
    y_prompt, ckv_list, kpe_list = trunk(
        x_prompt, c_ctx, None, None, None, w_mod, b_mod, norm1_g, norm2_g, w_in_ab, q_a_g, kv_a_g,
        w_q_b, w_kv_b, gmlp_v_g, w_spatial, b_spatial, w_out_ab, w_pool, pool_scale,
        w_ff1, w_ff2, final_g)
    new_ckv = jnp.stack(ckv_list, axis=1)
    new_kpe = jnp.stack(kpe_list, axis=1)
    ang = axial_angles(x_sample.shape[1])
    y_sample, _, _ = trunk(
        x_sample, c, ang, cache_ckv, cache_kpe, w_mod, b_mod, norm1_g, norm2_g, w_in_ab, q_a_g, kv_a_g,
        w_q_b, w_kv_b, gmlp_v_g, w_spatial, b_spatial, w_out_ab, w_pool, pool_scale,
        w_ff1, w_ff2, final_g)
    return (y_prompt, y_sample, new_ckv, new_kpe)
```

```python
import numpy as np
import concourse.bass as bass
import concourse.mybir as mybir
from concourse.bass_utils import run_bass_kernel_spmd

F32 = mybir.dt.float32
BF16 = mybir.dt.bfloat16
AF = mybir.ActivationFunctionType
ALU = mybir.AluOpType

GRAN = 256
SEMW = 4096
EPS = 1e-6
NT = 6
NTOK = 3072
ATT_SCALE = 96 ** -0.5
DBG_EVEN = 4
DBG_E1 = 9
DBG_NOCC = 0


def _esize(dt):
    return 2 if dt == BF16 else 4


class Buf:
    def __init__(self, handle, space, base, shape, dt, name):
        self.h = handle
        self.space = space
        self.base = base
        self.shape = list(shape)
        self.dt = dt
        self.es = _esize(dt)
        self.name = name
        self.rowlen = int(np.prod(shape[1:]))

    def __getitem__(self, idx):
        return V(self, self.h[idx])

    def ap(self, offset, pat):
        return V(self, bass.AP(tensor=self.h, offset=int(offset), ap=[list(p) for p in pat]))


class V:
    def __init__(self, buf, ap):
        self.buf = buf
        self.ap = ap

    def keys(self):
        b = self.buf
        if b.space == 'dram':
            return []
        pat = [list(p) for p in self.ap.ap]
        off = int(self.ap.offset)
        pcount = pat[0][1] if pat[0][0] != 0 else 1
        p0 = off // b.rowlen
        col0 = off % b.rowlen
        ext = 1
        for st, cnt in pat[1:]:
            ext += (cnt - 1) * st
        lo = b.base + col0 * b.es
        hi = b.base + (col0 + ext) * b.es
        q0 = p0 // 32
        q1 = (p0 + pcount - 1) // 32
        return [(b.space, g, q) for g in range(lo // GRAN, (hi - 1) // GRAN + 1) for q in range(q0, q1 + 1)]


class Op:
    __slots__ = ('eng', 'fn', 'kind', 'semkey', 'inc', 'ndma', 'signal', 'ticket', 'waits',
                 'deps_raw', 'deps_other', 'idx', 'sem')


class Sched:
    def __init__(self, nc):
        self.nc = nc
        self.ops = []
        self.lw = {}
        self.rd = {}
        self.last_dma = {}

    def add(self, eng, fn, reads=(), writes=(), kind='c', semkey=None, inc=16, ndma=1):
        o = Op()
        o.eng = eng; o.fn = fn; o.kind = kind; o.semkey = semkey; o.inc = inc; o.ndma = ndma
        o.signal = False; o.ticket = None; o.waits = []; o.idx = len(self.ops); o.sem = None
        raw = set(); oth = set()
        rk = []
        for r in reads:
            rk.extend(r.keys() if isinstance(r, V) else [r])
        wk = []
        for w in writes:
            wk.extend(w.keys() if isinstance(w, V) else [w])
        for k in rk:
            w = self.lw.get(k)
            if w is not None:
                raw.add(w)
        for k in wk:
            w = self.lw.get(k)
            if w is not None:
                oth.add(w)
            d = self.rd.get(k)
            if d:
                oth.update(d.values())
        ekey = eng if kind == 'c' else ('dma', semkey)
        for k in rk:
            self.rd.setdefault(k, {})[ekey] = o.idx
        for k in wk:
            self.lw[k] = o.idx
            self.rd[k] = {}
        if kind != 'c':
            p = self.last_dma.get(semkey)
            if p is not None:
                oth.add(p)
            self.last_dma[semkey] = o.idx
        raw.discard(o.idx); oth.discard(o.idx)
        o.deps_raw = raw
        o.deps_other = oth - raw
        self.ops.append(o)
        return o

    def finalize(self):
        nc = self.nc
        ops = self.ops
        need = [set() for _ in ops]
        for o in ops:
            need[o.idx].update(o.deps_raw)
            for d in o.deps_other:
                p = ops[d]
                if p.kind == 'c' and p.eng == o.eng and o.kind == 'c' and o.eng == 'pe':
                    continue
                need[o.idx].add(d)
        for o in ops:
            for d in need[o.idx]:
                if ops[d].fn is not None:
                    ops[d].signal = True
        cnt = {}
        self.sems = {}
        dcount = {}
        for o in ops:
            if o.fn is None:
                continue
            if o.kind == 'c':
                if o.signal:
                    t = cnt.get(o.eng, 0)
                    cnt[o.eng] = t + 1
                    key = (o.eng, t // SEMW)
                    if key not in self.sems:
                        self.sems[key] = nc.alloc_semaphore(name='s_%s_%d' % key)
                    o.sem = self.sems[key]
                    o.ticket = (key, t % SEMW + 1)
            else:
                key = ('dma', o.semkey)
                if key not in self.sems:
                    self.sems[key] = nc.alloc_semaphore(name='d_%d' % len(self.sems))
                o.sem = self.sems[key]
                c = dcount.get(key, 0) + o.inc * o.ndma
                dcount[key] = c
                o.ticket = (key, c)
        waited = {}
        maxwin = {}
        for o in ops:
            ws = {}
            for d in need[o.idx]:
                p = ops[d]
                if p.fn is None or p.ticket is None:
                    continue
                key, val = p.ticket
                if ws.get(key, 0) < val:
                    ws[key] = val
            wl = []
            for key, val in ws.items():
                wk = (o.eng, key)
                if waited.get(wk, 0) >= val:
                    continue
                if key[0] != 'dma':
                    if maxwin.get((o.eng, key[0]), -1) > key[1]:
                        continue
                    maxwin[(o.eng, key[0])] = max(maxwin.get((o.eng, key[0]), -1), key[1])
                waited[wk] = val
                wl.append((self.sems[key], val))
            o.waits = wl
        self.nsem = len(self.sems)

    def emit(self, block):
        decos = {'pe': block.tensor, 'act': block.scalar, 'dve': block.vector,
                 'pool': block.gpsimd, 'sp': block.sync}
        for ename, deco in decos.items():
            eops = [o for o in self.ops if o.eng == ename]
            if not eops:
                continue

            def body(e, eops=eops):
                for o in eops:
                    for (sem, val) in o.waits:
                        e.wait_ge(sem, val)
                    if o.fn is None:
                        continue
                    r = o.fn(e)
                    insts = list(r) if isinstance(r, (list, tuple)) else [r]
                    if o.kind != 'c':
                        for i in insts:
                            i.then_inc(o.sem, o.inc)
                    elif o.signal:
                        insts[-1].then_inc(o.sem, 1)
            deco(body)


class Stack:
    def __init__(self, K, lo, hi):
        self.K = K; self.lo = lo; self.hi = hi; self.cur = lo

    def alloc(self, name, shape, dt):
        nbytes = (int(np.prod(shape[1:])) * _esize(dt) + 31) // 32 * 32
        assert self.cur + nbytes <= self.hi, ('SBUF OOM', name, self.cur, nbytes, self.hi)
        self.K.nalloc += 1
        h = self.K.nc.alloc_sbuf_tensor_at('%s_%d' % (name, self.K.nalloc), list(shape), dt, offset=self.cur)
        b = Buf(h, 'sb', self.cur, shape, dt, name)
        self.cur += nbytes
        return b

    def sub(self):
        return Stack(self.K, self.cur, self.hi)


class Banks:
    def __init__(self, bufs):
        self.free = list(bufs)

    def get(self):
        assert self.free, 'PSUM banks exhausted'
        return self.free.pop(0)

    def put(self, b):
        self.free.append(b)


class K:
    def __init__(self, stages):
        self.stages = stages
        nc = self.nc = bass.Bass("TRN2", target_bir_lowering=False)
        self.S = Sched(nc)
        self.nalloc = 0
        lo = (nc._sbuf_addr_for_side('left') + 31) // 32 * 32
        hi = nc._sbuf_addr_for_side('right') // 32 * 32
        self.top = Stack(self, lo, hi)
        self.PS = [Buf(nc.alloc_psum_tensor('ps%d' % i, [128, 512], F32), 'ps', i * 2048, [128, 512], F32, 'ps%d' % i)
                   for i in range(8)]
        self.banks = Banks(self.PS)
        self.din = {}
        self.dout = {}
        self.outkeys = []
        self.nout = 0

    def inp(self, name, shape):
        t = self.nc.dram_tensor(name, list(shape), F32, kind="ExternalInput")
        b = Buf(t, 'dram', 0, shape, F32, name)
        self.din[name] = b
        return b

    def outp(self, name, shape):
        t = self.nc.dram_tensor(name, list(shape), F32, kind="ExternalOutput")
        b = Buf(t, 'dram', 0, shape, F32, name)
        self.dout[name] = b
        return b

    def scratch(self, name, shape, dt):
        t = self.nc.dram_tensor(name, list(shape), dt)
        return Buf(t, 'dram', 0, shape, dt, name)

    def dma(self, q, out, in_, semkey, rk=(), wk=()):
        reads = [in_] + list(rk)
        writes = [out] + list(wk)
        return self.S.add(q, lambda e: e.dma_start(out=out.ap, in_=in_.ap), reads=reads, writes=writes,
                          kind='dma', semkey=semkey)

    def dma_out(self, q, out, in_):
        self.nout += 1
        key = ('out', self.nout)
        self.outkeys.append(key)
        return self.dma(q, out, in_, semkey=('o', self.nout % 8), wk=[key])

    def mm(self, out, lhsT, rhs, start, stop):
        return self.S.add('pe', lambda e: e.matmul(out.ap, lhsT.ap, rhs.ap, start=start, stop=stop),
                          reads=[lhsT, rhs], writes=[out])

    def act(self, out, in_, func, scale=1.0, bias=0.0, accum=None, eng='act'):
        reads = [in_]
        writes = [out]
        kw = {}
        if isinstance(scale, V):
            reads.append(scale); kw['scale'] = scale.ap
        else:
            kw['scale'] = float(scale)
        if isinstance(bias, V):
            reads.append(bias); kw['bias'] = bias.ap
        elif bias != 0.0:
            kw['bias'] = float(bias)
        if accum is not None:
            writes.append(accum); kw['accum_out'] = accum.ap
        return self.S.add('act', lambda e: e.activation(out=out.ap, in_=in_.ap, func=func, **kw),
                          reads=reads, writes=writes)

    def tt(self, out, in0, in1, op, eng='dve'):
        return self.S.add(eng, lambda e: e.tensor_tensor(out=out.ap, in0=in0.ap, in1=in1.ap, op=op),
                          reads=[in0, in1], writes=[out])

    def ts(self, out, in0, s1, s2, op0, op1=None, eng='dve'):
        reads = [in0]
        a1 = s1.ap if isinstance(s1, V) else float(s1)
        if isinstance(s1, V):
            reads.append(s1)
        a2 = None
        if s2 is not None:
            a2 = s2.ap if isinstance(s2, V) else float(s2)
            if isinstance(s2, V):
                reads.append(s2)
        if op1 is None:
            return self.S.add(eng, lambda e: e.tensor_scalar(out=out.ap, in0=in0.ap, scalar1=a1, scalar2=None, op0=op0),
                              reads=reads, writes=[out])
        return self.S.add(eng, lambda e: e.tensor_scalar(out=out.ap, in0=in0.ap, scalar1=a1, scalar2=a2, op0=op0, op1=op1),
                          reads=reads, writes=[out])

    def stt(self, out, in0, scalar, in1, op0, op1, eng='dve'):
        reads = [in0, in1]
        sc = scalar.ap if isinstance(scalar, V) else float(scalar)
        if isinstance(scalar, V):
            reads.append(scalar)
        return self.S.add(eng, lambda e: e.scalar_tensor_tensor(out=out.ap, in0=in0.ap, scalar=sc, in1=in1.ap, op0=op0, op1=op1),
                          reads=reads, writes=[out])

    def copy(self, out, in_, eng='dve'):
        if eng == 'act':
            return self.act(out, in_, AF.Copy)
        return self.S.add(eng, lambda e: e.tensor_copy(out=out.ap, in_=in_.ap), reads=[in_], writes=[out])

    def memset(self, out, val, eng='dve'):
        return self.S.add(eng, lambda e: e.memset(out.ap, val), writes=[out])

    def recip(self, out, in_, eng='dve'):
        return self.S.add(eng, lambda e: e.reciprocal(out=out.ap, in_=in_.ap), reads=[in_], writes=[out])

    def gather(self, groups, src, dst, semkey, rk, wk):
        return self.S.add('pool', lambda e: e.collective_compute("AllGather", ALU.bypass, replica_groups=groups,
                                                                 ins=[src.ap], outs=[dst.ap]),
                          reads=list(rk), writes=list(wk), kind='cc', semkey=semkey, inc=1)

    def stats(self, srcs, n, RS, SQ):
        bank = self.banks.get()
        N = RS.ap.ap[-1][1]
        ps = bank[:, 0:N]
        for i, s in enumerate(srcs):
            sq = SQ[i % 2]
            sqv = sq[:, 0:N]
            self.act(sqv, s, AF.Square)
            self.mm(ps, self.ONES[:, :], sqv, start=(i == 0), stop=(i == len(srcs) - 1))
        self.ts(RS, ps, 1.0 / n, EPS, ALU.mult, ALU.add)
        self.banks.put(bank)
        self.act(RS, RS, AF.Sqrt)
        self.recip(RS, RS)

    def der(self, l, cond, kind, c):
        return self.DER[:, l, cond, kind, c:c + 1]

    def build(self):
        nc = self.nc
        top = self.top
        st = self.stages
        xT = self.inp('xT', [1024, NTOK])
        condT = self.inp('condT', [128, 40])
        sel = self.inp('sel', [128, 4])
        wmodr = self.inp('wmodr', [4, 1024, 768])
        bmodr = self.inp('bmodr', [128, 24])
        ng1 = self.inp('ng1', [128, 32])
        ng2 = self.inp('ng2', [128, 32])
        fg = self.inp('fg', [128, 8])
        psc = self.inp('psc', [128, 16])
        if any(x.startswith('ffn') for x in st):
            self.w_ff1 = self.inp('w_ff1', [4, 1024, 4096])
            self.w_ff2 = self.inp('w_ff2', [4, 4096, 1024])
        self.w_pool = self.inp('w_pool', [2, 4, 256, 256])
        self.pinv = self.inp('pinv', [128, 3 * 4 * 256])
        self.hmask = self.inp('hmask', [128, 2])
        self.w_in = self.inp('w_in', [2, 1024, 1792])
        self.wq = self.inp('wq', [2, 384, 8 * 192])
        self.wk = self.inp('wk', [2, 256, 8 * 128])
        self.wv = self.inp('wv', [2, 256, 8 * 64])
        self.wo = self.inp('wo', [2, 1024, 1024])
        self.wsT = self.inp('wsT', [2, 128, 8 * 128])
        self.bsb = self.inp('bsb', [2, 128, 4 * 128])
        self.vgb = self.inp('vgb', [2, 128, 512])
        self.qag = self.inp('qag', [128, 6])
        self.kvg = self.inp('kvg', [128, 4])
        self.cacheT = self.inp('cacheT', [2, 288, 512])
        self.ropeT = self.inp('ropeT', [2, 64, 2048])
        self.yT = self.outp('yT', [1024, NTOK])
        self.ckv_o = self.outp('ckv_o', [2, 256, 1024])
        self.kpe_o = self.outp('kpe_o', [2, 32, 1024])
        self.modb = self.scratch('modb', [128, 120], F32)
        self.modg = self.scratch('modg', [1024, 120], F32)
        self.hb = self.scratch('hb', [128, 128], F32)
        self.hg = self.scratch('hg', [256, 128], F32)
        self.kvb = [self.scratch('kvb%d' % j, [288, 512], F32) for j in range(4)]
        self.kvgd = [self.scratch('kvgd%d' % j, [576, 512], F32) for j in range(4)]
        self.qnb = self.scratch('qnb', [384, 2048], BF16)

        self.X = top.alloc('X', [128, 8, NTOK], F32)
        self.ONES = top.alloc('ONES', [128, 128], BF16)
        self.ONESF = top.alloc('ONESF', [128, 64], F32)
        self.DER = top.alloc('DER', [128, 4, 2, 6, 8], F32)
        self.FG = top.alloc('FG', [128, 8], F32)
        self.QAG = top.alloc('QAG', [128, 6], F32)
        self.KVG = top.alloc('KVG', [128, 4], F32)
        self.HM = top.alloc('HM', [128, 2], F32)
        X = self.X
        self.memset(self.ONES[:, :], 1.0)
        self.memset(self.ONESF[:, :], 1.0)
        for c in range(8):
            self.dma('sp', X[:, c, :], xT[c * 128:(c + 1) * 128, :], semkey=('X', c))
        self.dma('sp', self.FG[:, :], fg[:, :], semkey='small')
        self.dma('sp', self.QAG[:, :], self.qag[:, :], semkey='small')
        self.dma('sp', self.KVG[:, :], self.kvg[:, :], semkey='small')
        self.dma('sp', self.HM[:, :], self.hmask[:, :], semkey='small')
        self.setup_mod(top.sub(), condT, sel, wmodr, bmodr, ng1, ng2, psc)

        for l in range(4):
            if ('mix%d' % l) in st:
                if l % 2 == 0:
                    self.even_layer(l, top.sub())
                else:
                    self.odd_layer(l, top.sub())
            if ('ffn%d' % l) in st:
                self.ffn(l, top.sub())
        self.final(top.sub())
        self.S.add('sp', None, reads=list(self.outkeys))
        self.S.finalize()
        with nc.Block() as block:
            self.S.emit(block)
        return nc

    def setup_mod(self, stk, condT, sel, wmodr, bmodr, ng1, ng2, psc):
        CT = stk.alloc('CT', [128, 40], F32)
        ST = stk.alloc('ST', [128, 8, 5], BF16)
        SEL = stk.alloc('SEL', [128, 4], F32)
        BM = stk.alloc('BM', [128, 24], F32)
        NG1 = stk.alloc('NG1', [128, 4, 8], F32)
        NG2 = stk.alloc('NG2', [128, 4, 8], F32)
        PSC = stk.alloc('PSC', [128, 2, 8], F32)
        MODP = stk.alloc('MODP', [128, 120], F32)
        MODALL = stk.alloc('MODALL', [128, 8, 120], F32)
        MS = stk.alloc('MS', [128, 8, 24, 2], F32)
        WM = [stk.alloc('WM%d' % l, [128, 8, 768], BF16) for l in range(4)]
        self.dma('sp', CT[:, :], condT[:, :], semkey='small')
        self.dma('sp', SEL[:, :], sel[:, :], semkey='small')
        self.dma('sp', BM[:, :], bmodr[:, :], semkey='small')
        self.dma('sp', NG1[:, :, :], ng1.ap(0, [[32, 128], [8, 4], [1, 8]]), semkey='small')
        self.dma('sp', NG2[:, :, :], ng2.ap(0, [[32, 128], [8, 4], [1, 8]]), semkey='small')
        self.dma('sp', PSC[:, :, :], psc.ap(0, [[16, 128], [8, 2], [1, 8]]), semkey='small')
        for l in range(4):
            self.dma('pool', WM[l][:, :, :], wmodr.ap(l * 1024 * 768, [[768, 128], [128 * 768, 8], [1, 768]]),
                     semkey=('wm', l))
        self.act(ST.ap(0, [[40, 128], [1, 40]]), CT[:, :], AF.Silu)
        bank = self.banks.get()
        for lm in range(24):
            l, m = lm // 6, lm % 6
            for k in range(8):
                self.mm(bank[:, lm * 5:(lm + 1) * 5], WM[l][:, k, m * 128:(m + 1) * 128], ST[:, k, :],
                        start=(k == 0), stop=(k == 7))
        self.tt(MODP.ap(0, [[120, 128], [5, 24], [1, 5]]), bank.ap(0, [[512, 128], [5, 24], [1, 5]]),
                BM.ap(0, [[24, 128], [1, 24], [0, 5]]), ALU.add)
        self.banks.put(bank)
        self.dma('sp', self.modb[:, :], MODP[:, :], semkey='modb', wk=[('dram', 'modb')])
        self.gather([list(range(8))], self.modb[:, :], self.modg[:, :], semkey='cc_mod',
                    rk=[('dram', 'modb')], wk=[('dram', 'modg')])
        self.dma('sp', MODALL[:, :, :], self.modg.ap(0, [[120, 128], [128 * 120, 8], [1, 120]]), semkey='modall',
                 rk=[('dram', 'modg')])
        self.copy(MS.ap(0, [[384, 128], [48, 8], [2, 24]]), MODALL.ap(0, [[960, 128], [120, 8], [5, 24]]))
        ms1 = MS.ap(1, [[384, 128], [48, 8], [2, 24]])
        self.ts(ms1, MODALL.ap(1, [[960, 128], [120, 8], [5, 24]]), SEL[:, 0:1], None, ALU.mult)
        for j in range(1, 4):
            self.stt(ms1, MODALL.ap(1 + j, [[960, 128], [120, 8], [5, 24]]), SEL[:, j:j + 1], ms1, ALU.mult, ALU.add)
        for l in range(4):
            for cond in range(2):
                def msv(m):
                    return MS.ap((l * 6 + m) * 2 + cond, [[384, 128], [48, 8]])
                d = lambda kind: self.DER[:, l, cond, kind, :]
                self.stt(d(0), msv(1), 1.0, NG1[:, l, :], ALU.add, ALU.mult)
                self.copy(d(1), msv(0))
                if l % 2 == 1:
                    self.tt(d(2), msv(2), PSC[:, l // 2, :], ALU.mult)
                else:
                    self.copy(d(2), msv(2))
                self.stt(d(3), msv(4), 1.0, NG2[:, l, :], ALU.add, ALU.mult)
                self.copy(d(4), msv(3))
                self.copy(d(5), msv(5))

    def final(self, stk):
        X = self.X
        SQ = [stk.alloc('SQ%d' % i, [128, 512], BF16) for i in range(2)]
        RS = stk.alloc('RS', [128, 512], F32)
        YS = [stk.alloc('YS%d' % i, [128, 512], F32) for i in range(3)]
        n = 0
        for t in range(NT):
            tsl = slice(t * 512, (t + 1) * 512)
            self.stats([X[:, c, tsl] for c in range(8)], 1024, RS[:, :], SQ)
            for c in range(8):
                ys = YS[n % 3]; n += 1
                self.stt(ys[:, :], X[:, c, tsl], self.FG[:, c:c + 1], RS[:, :], ALU.mult, ALU.mult)
                self.dma_out('sp', self.yT[c * 128:(c + 1) * 128, tsl], ys[:, :])

    def ffn(self, l, stk):
        X = self.X
        H = stk.alloc('H', [128, 8, NTOK], BF16)
        W1 = [stk.alloc('W1_%d' % i, [128, 8, 512], BF16) for i in range(2)]
        W2 = [stk.alloc('W2_%d' % i, [128, 4, 1024], BF16) for i in range(2)]
        A = [stk.alloc('A%d' % i, [128, 4, 512], BF16) for i in range(2)]
        SQ = [stk.alloc('SQ%d' % i, [128, 512], BF16) for i in range(2)]
        RS = stk.alloc('RS', [128, 512], F32)
        TMP = [stk.alloc('TMP%d' % i, [128, 512], F32) for i in range(2)]

        def load(j):
            b = j % 2
            self.dma('pool', W1[b][:, :, :],
                     self.w_ff1.ap(l * 1024 * 4096 + j * 512, [[4096, 128], [128 * 4096, 8], [1, 512]]),
                     semkey=('w1', b))
            self.dma('pool', W2[b][:, :, :],
                     self.w_ff2.ap(l * 4096 * 1024 + j * 512 * 1024, [[1024, 128], [128 * 1024, 4], [1, 1024]]),
                     semkey=('w2', b))
        load(0)
        for t in range(NT):
            cond = 0 if t < 2 else 1
            tsl = slice(t * 512, (t + 1) * 512)
            self.stats([X[:, c, tsl] for c in range(8)], 1024, RS[:, :], SQ)
            for c in range(8):
                tmp = TMP[c % 2]
                self.tt(tmp[:, :], X[:, c, tsl], RS[:, :], ALU.mult)
                self.act(H[:, c, tsl], tmp[:, :], AF.Identity, scale=self.der(l, cond, 3, c), bias=self.der(l, cond, 4, c))
        na = 0
        for j in range(8):
            if j + 1 < 8:
                load(j + 1)
            b = j % 2
            for t in range(NT):
                cond = 0 if t < 2 else 1
                tsl = slice(t * 512, (t + 1) * 512)
                a = A[na % 2]; na += 1
                for fc in range(4):
                    bank = self.banks.get()
                    for k in range(8):
                        self.mm(bank[:, :], W1[b][:, k, fc * 128:(fc + 1) * 128], H[:, k, tsl], start=(k == 0), stop=(k == 7))
                    self.act(a[:, fc, :], bank[:, :], AF.Relu)
                    self.banks.put(bank)
                    self.tt(a[:, fc, :], a[:, fc, :], a[:, fc, :], ALU.mult)
                for oc in range(8):
                    bank = self.banks.get()
                    for fc in range(4):
                        self.mm(bank[:, :], W2[b][:, fc, oc * 128:(oc + 1) * 128], a[:, fc, :], start=(fc == 0), stop=(fc == 3))
                    self.stt(X[:, oc, tsl], bank[:, :], self.der(l, cond, 5, oc), X[:, oc, tsl], ALU.mult, ALU.add)
                    self.banks.put(bank)

    def odd_layer(self, l, stk):
        X = self.X
        jj = l // 2
        RSA = stk.alloc('RSA', [128, NTOK], F32)
        SQ = [stk.alloc('SQ%d' % i, [128, 512], BF16) for i in range(2)]
        HP = [stk.alloc('HP%d' % i, [128, 8, 272], F32) for i in range(2)]
        WP = stk.alloc('WP', [128, 4, 2, 256], BF16)
        PINV = stk.alloc('PINV', [128, 3, 4, 256], F32)
        TMP = [stk.alloc('TMP%d' % i, [128, 272], F32) for i in range(2)]
        SA = stk.alloc('SA', [128, 272], F32)
        SB = stk.alloc('SB', [128, 272], F32)
        DT = stk.alloc('DT', [128, 256], F32)
        DD = [stk.alloc('DD%d' % i, [128, 2, 256], BF16) for i in range(2)]
        T16 = stk.alloc('T16', [128, 16], F32)
        HB = stk.alloc('HB', [128, 8, 16], F32)
        HG = stk.alloc('HG', [128, 2, 8, 16], F32)
        LH = stk.alloc('LH', [128, 8, 8], F32)
        RH = stk.alloc('RH', [128, 8, 8], F32)
        self.dma('pool', WP[:, :, :, :], self.w_pool.ap(jj * 4 * 256 * 256, [[256, 128], [256 * 256, 4], [128 * 256, 2], [1, 256]]),
                 semkey='wp')
        self.dma('sp', PINV.ap(0, [[3072, 128], [1, 3072]]), self.pinv[:, :], semkey='pinv')
        for t in range(NT):
            tsl = slice(t * 512, (t + 1) * 512)
            self.stats([X[:, c, tsl] for c in range(8)], 1024, RSA[:, tsl], SQ)
        for c in range(8):
            self.tt(T16.ap(0, [[16, 128], [8, 2], [1, 8]]), X.ap(c * NTOK + 1024, [[8 * NTOK, 128], [2040, 2], [1, 8]]),
                    RSA.ap(1024, [[NTOK, 128], [2040, 2], [1, 8]]), ALU.mult)
            self.act(HB[:, c, :], T16[:, :], AF.Identity, scale=self.der(l, 1, 0, c), bias=self.der(l, 1, 1, c))
        self.dma('sp', self.hb[:, :], HB.ap(0, [[128, 128], [1, 128]]), semkey='hb', wk=[('dram', 'hb')])
        self.gather([[0, 1], [2, 3], [4, 5], [6, 7]], self.hb[:, :], self.hg[:, :], semkey='cc_h',
                    rk=[('dram', 'hb')], wk=[('dram', 'hg')])
        self.dma('sp', HG.ap(0, [[256, 128], [128, 2], [1, 128]]), self.hg.ap(0, [[128, 128], [128 * 128, 2], [1, 128]]),
                 semkey='hg', rk=[('dram', 'hg')])
        self.ts(LH[:, :, :], HG[:, 0, :, 8:16], self.HM[:, 0:1], None, ALU.mult)
        self.ts(RH[:, :, :], HG[:, 1, :, 0:8], self.HM[:, 1:2], None, ALU.mult)
        segs = [(False, s, s * 256) for s in range(4)] + [(True, s, 1024 + s * 256) for s in range(8)]
        ndd = 0
        for n, (smp, s, tok0) in enumerate(segs):
            hp = HP[n % 2]
            hprev = HP[(n - 1) % 2]
            cond = 1 if smp else 0
            ncol = 264 if (smp and s < 7) else 256
            for c in range(8):
                tmp = TMP[c % 2]
                self.tt(tmp[:, 0:ncol], X[:, c, tok0:tok0 + ncol], RSA[:, tok0:tok0 + ncol], ALU.mult)
                self.act(hp[:, c, 8:8 + ncol], tmp[:, 0:ncol], AF.Identity, scale=self.der(l, cond, 0, c), bias=self.der(l, cond, 1, c))
            if not smp:
                self.memset(hp[:, :, 0:8], 0.0)
                self.memset(hp[:, :, 264:272], 0.0)
                kind = 0
            else:
                if s == 0:
                    self.copy(hp[:, :, 0:8], LH[:, :, :])
                else:
                    self.copy(hp[:, :, 0:8], hprev[:, :, 256:264])
                if s == 7:
                    self.copy(hp[:, :, 264:272], RH[:, :, :])
                kind = 1 if s == 0 else (2 if s == 7 else None)
            for gi, w in enumerate((2, 4, 8, 16)):
                dd = DD[ndd % 2]; ndd += 1
                for kc in range(2):
                    c = 2 * gi + kc
                    hv = hp[:, c, :]
                    hvb = lambda a, b: hp[:, c, a:b]
                    self.tt(SA[:, 1:272], hvb(0, 271), hvb(1, 272), ALU.add)
                    Fb = SA
                    if w >= 4:
                        self.tt(SB[:, 2:271], SA[:, 1:270], SA[:, 3:272], ALU.add); Fb = SB
                    if w >= 8:
                        self.tt(SA[:, 4:269], SB[:, 2:267], SB[:, 6:271], ALU.add); Fb = SA
                    if w >= 16:
                        self.tt(SB[:, 8:265], SA[:, 4:261], SA[:, 12:269], ALU.add); Fb = SB
                    if kind is None:
                        self.stt(dd[:, kc, :], Fb[:, 8:264], 1.0 / w, hvb(8, 264), ALU.mult, ALU.subtract)
                    else:
                        self.tt(DT[:, :], Fb[:, 8:264], PINV[:, kind, gi, :], ALU.mult)
                        self.tt(dd[:, kc, :], DT[:, :], hvb(8, 264), ALU.subtract)
                for oc in range(2):
                    co = 2 * gi + oc
                    bank = self.banks.get()
                    for kc in range(2):
                        self.mm(bank[:, 0:256], WP[:, gi, kc, oc * 128:(oc + 1) * 128], dd[:, kc, :], start=(kc == 0), stop=(kc == 1))
                    self.stt(X[:, co, tok0:tok0 + 256], bank[:, 0:256], self.der(l, cond, 2, co), X[:, co, tok0:tok0 + 256], ALU.mult, ALU.add)
                    self.banks.put(bank)

    def attention(self, l, i, stk, L, CKV, KPE, qtiles, WK, WV, WQh, WOAh):
        X = self.X
        nqmax = max(q[1] for q in qtiles)
        nch = L // 128
        KH = [stk.alloc('KH%d' % j, [128, L], BF16) for j in range(2)]
        VH = [stk.alloc('VH%d' % j, [128, nch, 72], BF16) for j in range(2)]
        QH = [stk.alloc('QH%d' % j, [128, nqmax], BF16) for j in range(2)]
        LA = 3 if nch > 3 else 1
        NPT = LA + 2
        PT = [stk.alloc('PT%d' % j, [128, nqmax], BF16) for j in range(NPT)]
        OS = stk.alloc('OS', [65, nqmax], F32)
        RC = stk.alloc('RC', [65, nqmax], F32)
        AT = [stk.alloc('AT%d' % j, [64, nqmax], BF16) for j in range(2)]
        rope_any = any(q[3] is not None for q in qtiles)
        if rope_any:
            T1 = stk.alloc('T1', [64, nqmax], F32)
            T2 = stk.alloc('T2', [64, nqmax], F32)
        for j in range(2):
            self.memset(KH[j][0:32, :], 0.0)
            self.memset(QH[j][0:32, :], 0.0)
            self.memset(VH[j][:, :, 64:65], 1.0)
        nqh = 0
        ncp = 0
        for h in range(8):
            wq = WQh[h % 2]
            woa = WOAh[h % 2]
            self.dma('pool', wq[:, :, :], self.wq.ap(i * 384 * 1536 + h * 192, [[1536, 128], [128 * 1536, 3], [1, 192]]),
                     semkey=('wq', h % 2))
            self.dma('pool', woa[0:64, :], self.wo.ap(i * 1024 * 1024 + h * 64 * 1024, [[1024, 64], [1, 1024]]),
                     semkey=('woa', h % 2))
            kh = KH[h % 2]
            vh = VH[h % 2]
            for kt0 in range(0, L, 512):
                n = min(512, L - kt0)
                bank = self.banks.get()
                for kc in range(2):
                    self.mm(bank[:, 0:n], WK[:, kc, h * 128:(h + 1) * 128], CKV[:, kc, kt0:kt0 + n], start=(kc == 0), stop=(kc == 1))
                self.copy(kh[64:128, kt0:kt0 + n], bank[64:128, 0:n], eng='act')
                self.banks.put(bank)
                self.copy(kh[32:64, kt0:kt0 + n], KPE[32:64, kt0:kt0 + n], eng='dve')
            for c0 in range(0, nch, 8):
                nb = min(8, nch - c0)
                bank = self.banks.get()
                for ci in range(nb):
                    for kc in range(2):
                        self.mm(bank[:, ci * 64:(ci + 1) * 64], CKV[:, kc, (c0 + ci) * 128:(c0 + ci + 1) * 128],
                                WV[:, kc, h * 64:(h + 1) * 64], start=(kc == 0), stop=(kc == 1))
                self.copy(vh[:, c0:c0 + nb, 0:64], bank.ap(0, [[512, 128], [64, nb], [1, 64]]), eng=('act' if ncp % 2 else 'dve'))
                ncp += 1
                self.banks.put(bank)
            for (qnf, nq, xtok0, rope, cond) in qtiles:
                qh = QH[nqh % 2]
                at = AT[nqh % 2]
                nqh += 1
                bankA = self.banks.get()
                for kc in range(3):
                    self.mm(bankA[:, 0:nq], wq[:, kc, 0:128], qnf(kc), start=(kc == 0), stop=(kc == 2))
                self.copy(qh[64:128, 0:nq], bankA[64:128, 0:nq], eng='act')
                if rope is not None:
                    bankB = self.banks.get()
                    for kc in range(3):
                        self.mm(bankB[0:64, 0:nq], wq[:, kc, 128:192], qnf(kc), start=(kc == 0), stop=(kc == 2))
                    self.tt(T1[32:64, 0:nq], bankA[32:64, 0:nq], rope[0], ALU.mult)
                    self.tt(T2[32:64, 0:nq], bankB[32:64, 0:nq], rope[1], ALU.mult)
                    self.banks.put(bankB)
                    self.tt(qh[32:64, 0:nq], T1[32:64, 0:nq], T2[32:64, 0:nq], ALU.add)
                else:
                    self.copy(qh[32:64, 0:nq], bankA[32:64, 0:nq], eng='act')
                self.banks.put(bankA)
                obank = self.banks.get()
                sbs = {}

                def issue_s(kc):
                    sb = self.banks.get()
                    sbs[kc] = sb
                    self.mm(sb[:, 0:nq], kh[:, kc * 128:(kc + 1) * 128], qh[:, 0:nq], start=True, stop=True)
                for kk in range(min(LA, nch)):
                    issue_s(kk)
                for kc in range(nch):
                    if kc + LA < nch:
                        issue_s(kc + LA)
                    pt = PT[kc % NPT]
                    sb = sbs.pop(kc)
                    self.act(pt[:, 0:nq], sb[:, 0:nq], AF.Exp, scale=ATT_SCALE)
                    self.banks.put(sb)
                    self.mm(obank[0:65, 0:nq], vh[:, kc, 0:65], pt[:, 0:nq], start=(kc == 0), stop=(kc == nch - 1))
                self.copy(OS[0:65, 0:nq], obank[0:65, 0:nq], eng='act')
                self.banks.put(obank)
                self.recip(RC[64:65, 0:nq], OS[64:65, 0:nq])
                bb = self.banks.get()
                self.mm(bb[0:64, 0:nq], self.ONESF[64:65, 0:64], RC[64:65, 0:nq], start=True, stop=True)
                self.tt(at[0:64, 0:nq], OS[0:64, 0:nq], bb[0:64, 0:nq], ALU.mult)
                self.banks.put(bb)
                for oc in range(8):
                    wb = self.banks.get()
                    self.mm(wb[:, 0:nq], woa[0:64, oc * 128:(oc + 1) * 128], at[0:64, 0:nq], start=True, stop=True)
                    xs = X[:, oc, xtok0:xtok0 + nq]
                    self.stt(xs, wb[:, 0:nq], self.der(l, cond, 2, oc), xs, ALU.mult, ALU.add)
                    self.banks.put(wb)

    def even_layer(self, l, stk):
        X = self.X
        i = l // 2
        WK = stk.alloc('WK', [128, 2, 1024], BF16)
        WV = stk.alloc('WV', [128, 2, 512], BF16)
        WQh = [stk.alloc('WQh%d' % j, [128, 3, 192], BF16) for j in range(2)]
        WOAh = [stk.alloc('WOAh%d' % j, [64, 1024], BF16) for j in range(2)]
        ROPE = stk.alloc('ROPE', [64, 2, 2048], BF16)
        self.dma('pool', WK[:, :, :], self.wk.ap(i * 256 * 1024, [[1024, 128], [128 * 1024, 2], [1, 1024]]), semkey='wk')
        self.dma('pool', WV[:, :, :], self.wv.ap(i * 256 * 512, [[512, 128], [128 * 512, 2], [1, 512]]), semkey='wv')

        def load_rope():
            for j in range(2):
                self.dma('pool', ROPE[32:64, j, :], self.ropeT[j, 32:64, :], semkey='rope')
        load_rope()
        e1 = stk.sub()
        WIN = e1.alloc('WIN', [128, 8, 1792], BF16)
        WOG = e1.alloc('WOG', [128, 4, 1024], BF16)
        WST = e1.alloc('WST', [128, 8, 128], BF16)
        BSB = e1.alloc('BSB', [128, 4, 128], F32)
        VGB = e1.alloc('VGB', [128, 512], F32)
        SQ = [e1.alloc('SQ%d' % j, [128, 512], BF16) for j in range(2)]
        RS = e1.alloc('RS', [128, 512], F32)
        TMP = [e1.alloc('TMP%d' % j, [128, 512], F32) for j in range(2)]
        H = e1.alloc('H', [128, 8, 512], BF16)
        RSQ = e1.alloc('RSQ', [128, 512], F32)
        QNS = e1.alloc('QNS', [128, 3, 512], BF16)
        C32 = [e1.alloc('C32_%d' % j, [128, 512], F32) for j in range(2)]
        CKV16 = e1.alloc('CKV16', [128, 2, 512], BF16)
        KPE16 = e1.alloc('KPE16', [64, 512], BF16)
        U = e1.alloc('U', [128, 4, 512], BF16)
        GV = e1.alloc('GV', [128, 512], F32)
        TS_ = e1.alloc('TS', [128, 512], F32)
        SSV = e1.alloc('SSV', [128, 8], F32)
        VNP = [e1.alloc('VNP%d' % j, [128, 8, 128], BF16) for j in range(2)]
        self.dma('pool', WIN[:, :, :], self.w_in.ap(i * 1024 * 1792, [[1792, 128], [128 * 1792, 8], [1, 1792]]), semkey='win')
        self.dma('pool', WOG[:, :, :], self.wo.ap(i * 1024 * 1024 + 512 * 1024, [[1024, 128], [128 * 1024, 4], [1, 1024]]), semkey='wog')
        self.dma('pool', WST.ap(0, [[1024, 128], [1, 1024]]), self.wsT[i, :, :], semkey='wst')
        self.dma('sp', BSB.ap(0, [[512, 128], [1, 512]]), self.bsb[i, :, :], semkey='bsb')
        self.dma('sp', VGB[:, :], self.vgb[i, :, :], semkey='vgb')
        for j in range(2):
            self.memset(VNP[j][:, :, :], 0.0)
        n32 = [0]

        def e1_tile(t):
            cond = 0 if t < 2 else 1
            smp = t >= 2
            tsl = slice(t * 512, (t + 1) * 512)
            self.stats([X[:, c, tsl] for c in range(8)], 1024, RS[:, :], SQ)
            for c in range(8):
                tmp = TMP[c % 2]
                self.tt(tmp[:, :], X[:, c, tsl], RS[:, :], ALU.mult)
                self.act(H[:, c, :], tmp[:, :], AF.Identity, scale=self.der(l, cond, 0, c), bias=self.der(l, cond, 1, c))

            def gelu(out, ps, tmp):
                self.act(tmp, ps, AF.Square, scale=0.21145921592861536)
                self.stt(tmp, tmp, 1.0, ps, ALU.add, ALU.mult)
                self.act(tmp, tmp, AF.Exp, scale=-1.5957691216057308)
                self.ts(tmp, tmp, 1.0, None, ALU.add)
                self.recip(tmp, tmp)
                self.tt(out, tmp, ps, ALU.mult)

            def proj(col0, m, bank_rows=None):
                bank = self.banks.get()
                for k in range(8):
                    self.mm(bank[0:m, :], WIN[:, k, col0:col0 + m], H[:, k, :], start=(k == 0), stop=(k == 7))
                return bank
            if DBG_E1 < 2:
                return
            qb = [proj(c * 128, 128) for c in range(3)]
            self.stats([b[:, :] for b in qb], 384, RSQ[:, :], SQ)
            for c in range(3):
                self.stt(QNS[:, c, :], qb[c][:, :], self.QAG[:, i * 3 + c:i * 3 + c + 1], RSQ[:, :], ALU.mult, ALU.mult)
                self.banks.put(qb[c])
            if smp:
                for c in range(3):
                    self.dma('sp', self.qnb[c * 128:(c + 1) * 128, (t - 2) * 512:(t - 1) * 512], QNS[:, c, :], semkey='qnb',
                             wk=[('dram', 'qnb', t)])
            if DBG_E1 < 3:
                return
            kb = [proj(384 + c * 128, 128) for c in range(2)]
            self.stats([b[:, :] for b in kb], 256, RSQ[:, :], SQ)
            for c in range(2):
                c32 = C32[n32[0] % 2]; n32[0] += 1
                self.stt(c32[:, :], kb[c][:, :], self.KVG[:, i * 2 + c:i * 2 + c + 1], RSQ[:, :], ALU.mult, ALU.mult)
                self.banks.put(kb[c])
                self.copy(CKV16[:, c, :], c32[:, :], eng='act')
                if smp:
                    self.dma('sp', self.kvb[t - 2][c * 128:(c + 1) * 128, :], c32[:, :], semkey='kvb',
                             wk=[('dram', 'kvb', t, c)])
                else:
                    self.dma_out('sp', self.ckv_o[i, c * 128:(c + 1) * 128, t * 512:(t + 1) * 512], c32[:, :])
            if DBG_E1 < 4:
                return
            pb = proj(1664, 64)
            if smp:
                pb2 = proj(1728, 64)
                tcol = slice((t - 2) * 512, (t - 1) * 512)
                self.tt(GV[32:64, :], pb[32:64, :], ROPE[32:64, 0, tcol], ALU.mult)
                self.tt(TS_[32:64, :], pb2[32:64, :], ROPE[32:64, 1, tcol], ALU.mult)
                self.banks.put(pb); self.banks.put(pb2)
                self.tt(GV[32:64, :], GV[32:64, :], TS_[32:64, :], ALU.add)
                self.dma('sp', self.kvb[t - 2][256:288, :], GV[32:64, :], semkey='kvb',
                         wk=[('dram', 'kvb', t, 2)])
                self.gather([[0, 1], [2, 3], [4, 5], [6, 7]], self.kvb[t - 2][:, :], self.kvgd[t - 2][:, :], semkey='cc_kv',
                            rk=[('dram', 'kvb', t, c) for c in range(3)], wk=[('dram', 'kvgd', t)])
            else:
                self.copy(GV[32:64, :], pb[32:64, :], eng='act')
                self.banks.put(pb)
                self.copy(KPE16[32:64, :], GV[32:64, :], eng='dve')
                self.dma_out('sp', self.kpe_o[i, :, t * 512:(t + 1) * 512], GV[32:64, :])
            if DBG_E1 < 5:
                return
            for c in range(4):
                ub = proj(640 + c * 128, 128)
                gelu(U[:, c, :], ub[:, :], TMP[c % 2][:, :])
                self.banks.put(ub)
            if DBG_E1 < 6:
                return
            sbanks = [self.banks.get() for _ in range(4)]
            for sc in range(4):
                vb = self.banks.get()
                for k in range(8):
                    self.mm(vb[:, :], H[:, k, sc * 128:(sc + 1) * 128], WIN[:, k, 1152:1664], start=(k == 0), stop=(k == 7))
                gelu(GV[:, :], vb[:, :], TMP[sc % 2][:, :])
                self.banks.put(vb)
                ssv = SSV[:, sc:sc + 1]
                self.act(TS_[:, :], GV[:, :], AF.Square, accum=ssv)
                self.ts(ssv, ssv, 1.0 / 512, EPS, ALU.mult, ALU.add)
                self.act(ssv, ssv, AF.Sqrt)
                self.recip(ssv, ssv)
                vnp = VNP[sc % 2]
                for par in range(2):
                    self.stt(vnp.ap(par * 192, [[1024, 128], [256, 4], [1, 64]]), GV.ap(par * 64, [[512, 128], [128, 4], [1, 64]]),
                             ssv, VGB.ap(par * 64, [[512, 128], [128, 4], [1, 64]]), ALU.mult, ALU.mult)
                for gp in range(4):
                    for par in range(2):
                        g = 2 * gp + par
                        self.mm(sbanks[gp][:, sc * 128:(sc + 1) * 128], vnp[:, g, :], WST[:, g, :], start=(par == 0), stop=(par == 1))
            for gp in range(4):
                self.tt(TS_.ap(0, [[512, 128], [128, 4], [1, 128]]), sbanks[gp].ap(0, [[512, 128], [128, 4], [1, 128]]),
                        BSB.ap(gp * 128, [[512, 128], [0, 4], [1, 128]]), ALU.add)
                self.banks.put(sbanks[gp])
                self.tt(U[:, gp, :], TS_[:, :], U[:, gp, :], ALU.mult)
            if DBG_E1 < 7:
                return
            for oc in range(8):
                wb = self.banks.get()
                for kc in range(4):
                    self.mm(wb[:, :], WOG[:, kc, oc * 128:(oc + 1) * 128], U[:, kc, :], start=(kc == 0), stop=(kc == 3))
                self.stt(X[:, oc, tsl], wb[:, :], self.der(l, cond, 2, oc), X[:, oc, tsl], ALU.mult, ALU.add)
                self.banks.put(wb)

        for t in range(2, 6):
            e1_tile(t)
        if DBG_EVEN < 2:
            return
        pa = Stack(self, ROPE.base, ROPE.base + 8192)
        for t in range(2 if DBG_EVEN >= 3 else 0):
            e1_tile(t)
            for sq in range(2):
                o = sq * 256
                CKVp = Buf(CKV16.h, 'sb', CKV16.base, CKV16.shape, BF16, 'CKVp')
                ckv_v = lambda kcs, a, b, o=o: None
                qt = [((lambda kc, o=o: QNS[:, kc, o:o + 256]), 256, t * 512 + o, None, 0)]
                self.attention_ctx(l, i, Stack(self, pa.lo, pa.hi), 256, CKV16, KPE16, o, qt, WK, WV, WQh, WOAh)
        sa = stk.sub()
        CKV = sa.alloc('CKV', [128, 2, 4608], BF16)
        KPE = sa.alloc('KPE', [64, 4608], BF16)
        QN = sa.alloc('QN', [128, 3, 2048], BF16)
        load_rope()
        for kc in range(2):
            self.dma('pool', CKV[:, kc, 0:512], self.cacheT[i, kc * 128:(kc + 1) * 128, :], semkey='ckvl')
            for r in range(2):
                for j in range(4):
                    k0 = 512 + r * 2048 + j * 512
                    self.dma('pool', CKV[:, kc, k0:k0 + 512], self.kvgd[j][r * 288 + kc * 128:r * 288 + (kc + 1) * 128, :],
                             semkey='ckvl2', rk=[('dram', 'kvgd', j + 2)])
        self.dma('pool', KPE[32:64, 0:512], self.cacheT[i, 256:288, :], semkey='ckvl')
        for r in range(2):
            for j in range(4):
                k0 = 512 + r * 2048 + j * 512
                self.dma('pool', KPE[32:64, k0:k0 + 512], self.kvgd[j][r * 288 + 256:r * 288 + 288, :],
                         semkey='ckvl2', rk=[('dram', 'kvgd', j + 2)])
        for kc in range(3):
            self.dma('sp', QN[:, kc, :], self.qnb[kc * 128:(kc + 1) * 128, :], semkey='qnl',
                     rk=[('dram', 'qnb', t) for t in range(2, 6)])
        qts = []
        for qi in range(4):
            qts.append(((lambda kc, qi=qi: QN[:, kc, qi * 512:(qi + 1) * 512]), 512, 1024 + qi * 512,
                        (ROPE[32:64, 0, qi * 512:(qi + 1) * 512], ROPE[32:64, 1, qi * 512:(qi + 1) * 512]), 1))
        if DBG_EVEN >= 4:
            self.attention_ctx(l, i, sa, 4608, CKV, KPE, 0, qts, WK, WV, WQh, WOAh)

    def attention_ctx(self, l, i, stk, L, CKV, KPE, koff, qtiles, WK, WV, WQh, WOAh):
        class Off:
            def __init__(s, b, off, three):
                s.b = b; s.off = off; s.three = three

            def __getitem__(s, idx):
                if s.three:
                    p, kc, cs = idx
                    return s.b[p, kc, slice(cs.start + s.off, cs.stop + s.off)]
                p, cs = idx
                return s.b[p, slice(cs.start + s.off, cs.stop + s.off)]
        self.attention(l, i, stk, L, Off(CKV, koff, True), Off(KPE, koff, False), qtiles, WK, WV, WQh, WOAh)


def build_nc(stages):
    k = K(stages)
    nc = k.build()
    return nc, k


ALL_STAGES = ('mix0', 'ffn0', 'mix1', 'ffn1', 'mix2', 'ffn2', 'mix3', 'ffn3')
_CACHE = {}


def prep_inputs(inp):
    f = lambda a: np.ascontiguousarray(a, dtype=np.float32)
    x_prompt = np.asarray(inp['x_prompt']); x_sample = np.asarray(inp['x_sample'])
    c = np.asarray(inp['c']); c_ctx = np.asarray(inp['c_ctx'])
    w_mod = np.asarray(inp['w_mod']); b_mod = np.asarray(inp['b_mod'])
    C5 = np.concatenate([c_ctx[None, :], c], axis=0)
    condT = f(C5.reshape(5, 8, 128).transpose(2, 1, 0).reshape(128, 40))
    fm = lambda v: v.reshape(-1, 128).T
    ng1 = f(np.stack([fm(np.asarray(inp['norm1_g'])[l]) for l in range(4)], 1).reshape(128, 32))
    ng2 = f(np.stack([fm(np.asarray(inp['norm2_g'])[l]) for l in range(4)], 1).reshape(128, 32))
    fg = f(fm(np.asarray(inp['final_g'])))
    psc = f(np.stack([fm(np.asarray(inp['pool_scale'])[j]) for j in range(2)], 1).reshape(128, 16))
    qag = f(np.stack([fm(np.asarray(inp['q_a_g'])[i]) for i in range(2)], 1).reshape(128, 6))
    kvg = f(np.stack([fm(np.asarray(inp['kv_a_g'])[i]) for i in range(2)], 1).reshape(128, 4))
    w_in_ab = np.asarray(inp['w_in_ab'])
    d = np.arange(32)
    swap = (d // 16) * 16 + (1 - (d % 16) // 8) * 8 + d % 8
    z32 = np.zeros((2, 1024, 32), np.float32)
    kpe = w_in_ab[:, :, 640:672]
    w_in = f(np.concatenate([w_in_ab[:, :, 0:640], w_in_ab[:, :, 672:1696], z32, kpe, z32, kpe[:, :, swap]], axis=2))
    w_q_b = np.asarray(inp['w_q_b']).reshape(2, 384, 8, 96)
    zq = np.zeros((2, 384, 8, 32), np.float32)
    rope = w_q_b[..., 64:96]
    wq = f(np.concatenate([zq, rope, w_q_b[..., 0:64], zq, rope[..., swap]], axis=3).reshape(2, 384, 8 * 192))
    w_kv_b = np.asarray(inp['w_kv_b']).reshape(2, 256, 8, 128)
    wk = f(np.concatenate([np.zeros((2, 256, 8, 64), np.float32), w_kv_b[..., 0:64]], axis=3).reshape(2, 256, 1024))
    wv = f(w_kv_b[..., 64:128].reshape(2, 256, 512))
    wo = f(inp['w_out_ab'])
    wsT = f(np.asarray(inp['w_spatial']).transpose(0, 3, 1, 2).reshape(2, 128, 8 * 128))
    b_s = np.asarray(inp['b_spatial'])
    bsb = f(np.broadcast_to(b_s.reshape(2, 4, 2, 1, 128), (2, 4, 2, 64, 128)).transpose(0, 2, 3, 1, 4).reshape(2, 128, 512))
    vgb = f(np.broadcast_to(np.asarray(inp['gmlp_v_g'])[:, None, :], (2, 128, 512)))
    inv = (10000.0 ** (-np.arange(0, 16, 2, dtype=np.float32) / 16)).astype(np.float32)
    shared = dict(condT=condT, ng1=ng1, ng2=ng2, fg=fg, psc=psc, qag=qag, kvg=kvg, w_in=w_in, wq=wq, wk=wk, wv=wv,
                  wo=wo, wsT=wsT, bsb=bsb, vgb=vgb, w_ff1=f(inp['w_ff1']), w_ff2=f(inp['w_ff2']),
                  w_pool=f(inp['w_pool']))
    cache_ckv = np.asarray(inp['cache_ckv']); cache_kpe = np.asarray(inp['cache_kpe'])
    maps = []
    for r in range(8):
        b, half = r // 2, r % 2
        xp = x_prompt[4 * r:4 * r + 4].reshape(1024, 1024)
        xs = x_sample[b, half * 2048:(half + 1) * 2048]
        m = dict(shared)
        m['xT'] = f(np.concatenate([xp.T, xs.T], axis=1))
        s = np.zeros((128, 4), np.float32); s[:, b] = 1.0
        m['sel'] = s
        wm = w_mod.reshape(4, 1024, 6, 8, 128)[:, :, :, r, :]
        m['wmodr'] = f(wm.reshape(4, 1024, 768))
        bm = b_mod.reshape(4, 6, 8, 128)[:, :, r, :]
        m['bmodr'] = f(bm.reshape(24, 128).T)
        m['cacheT'] = f(np.concatenate([cache_ckv[b].transpose(0, 2, 1), cache_kpe[b].transpose(0, 2, 1)], axis=1))
        pos = np.arange(half * 2048, (half + 1) * 2048)
        row = (pos // 64).astype(np.float32); col = (pos % 64).astype(np.float32)
        ang = np.stack([row[:, None] * inv[None, :], col[:, None] * inv[None, :]], axis=1)
        cosd = np.cos(ang)[:, d // 16, d % 8]
        sind = np.sin(ang)[:, d // 16, d % 8] * np.where((d % 16) // 8 == 0, -1.0, 1.0)[None, :]
        rp = np.zeros((2, 64, 2048), np.float32)
        rp[0, 32:64] = cosd.T; rp[1, 32:64] = sind.T
        m['ropeT'] = rp
        pin = np.zeros((3, 4, 256), np.float32)
        for gi, w in enumerate((2, 4, 8, 16)):
            tt_ = np.arange(256)
            cntP = np.clip(tt_ + w // 2, 0, 256) - np.clip(tt_ - w // 2, 0, 256)
            pin[0, gi] = 1.0 / cntP
            ta = pos[:256]
            pin[1, gi] = 1.0 / (np.clip(ta + w // 2, 0, 4096) - np.clip(ta - w // 2, 0, 4096))
            tb = pos[-256:]
            pin[2, gi] = 1.0 / (np.clip(tb + w // 2, 0, 4096) - np.clip(tb - w // 2, 0, 4096))
        m['pinv'] = f(np.broadcast_to(pin.reshape(1, -1), (128, 3 * 4 * 256)))
        hm = np.zeros((128, 2), np.float32); hm[:, 0] = float(half == 1); hm[:, 1] = float(half == 0)
        m['hmask'] = hm
        maps.append(m)
    return maps


def assemble(results):
    y_prompt = np.zeros((32, 256, 1024), np.float32)
    y_sample = np.zeros((4, 4096, 1024), np.float32)
    new_ckv = np.zeros((32, 2, 256, 256), np.float32)
    new_kpe = np.zeros((32, 2, 256, 32), np.float32)
    for r in range(8):
        b, half = r // 2, r % 2
        yT = np.asarray(results[r]['yT'])
        y_prompt[4 * r:4 * r + 4] = yT[:, :1024].T.reshape(4, 256, 1024)
        y_sample[b, half * 2048:(half + 1) * 2048] = yT[:, 1024:].T
        ck = np.asarray(results[r]['ckv_o'])
        kp = np.asarray(results[r]['kpe_o'])
        new_ckv[4 * r:4 * r + 4] = ck.transpose(2, 0, 1).reshape(4, 256, 2, 256).transpose(0, 2, 1, 3)
        new_kpe[4 * r:4 * r + 4] = kp.transpose(2, 0, 1).reshape(4, 256, 2, 32).transpose(0, 2, 1, 3)
    return y_prompt, y_sample, new_ckv, new_kpe


def kernel(**inputs):
    if 'nc' not in _CACHE:
        _CACHE['nc'] = build_nc(set(ALL_STAGES))[0]
    maps = prep_inputs(inputs)
    res = run_bass_kernel_spmd(_CACHE['nc'], maps, core_ids=list(range(8)))
    return assemble(res.results)
```

```python
import numpy as np
import concourse.bass as bass
import concourse.mybir as mybir
from concourse.bass_utils import run_bass_kernel_spmd

F32 = mybir.dt.float32
BF16 = mybir.dt.bfloat16
AF = mybir.ActivationFunctionType
ALU = mybir.AluOpType

GRAN = 256
SEMW = 4096
EPS = 1e-6
NT = 6
NTOK = 3072
ATT_SCALE = 96 ** -0.5
FUSE_WAITS = True
DBG_EVEN = 4
DBG_E1 = 9
DBG_NOCC = 0


def _esize(dt):
    return 2 if dt == BF16 else 4


class Buf:
    def __init__(self, handle, space, base, shape, dt, name):
        self.h = handle
        self.space = space
        self.base = base
        self.shape = list(shape)
        self.dt = dt
        self.es = _esize(dt)
        self.name = name
        self.rowlen = int(np.prod(shape[1:]))

    def __getitem__(self, idx):
        return V(self, self.h[idx])

    def ap(self, offset, pat):
        return V(self, bass.AP(tensor=self.h, offset=int(offset), ap=[list(p) for p in pat]))


class V:
    def __init__(self, buf, ap):
        self.buf = buf
        self.ap = ap

    def keys(self):
        b = self.buf
        if b.space == 'dram':
            return []
        pat = [list(p) for p in self.ap.ap]
        off = int(self.ap.offset)
        pcount = pat[0][1] if pat[0][0] != 0 else 1
        p0 = off // b.rowlen
        col0 = off % b.rowlen
        ext = 1
        for st, cnt in pat[1:]:
            ext += (cnt - 1) * st
        lo = b.base + col0 * b.es
        hi = b.base + (col0 + ext) * b.es
        q0 = p0 // 32
        q1 = (p0 + pcount - 1) // 32
        return [(b.space, g, q) for g in range(lo // GRAN, (hi - 1) // GRAN + 1) for q in range(q0, q1 + 1)]


class Op:
    __slots__ = ('eng', 'fn', 'kind', 'semkey', 'inc', 'ndma', 'signal', 'ticket', 'waits',
                 'deps_raw', 'deps_other', 'idx', 'sem')


class Sched:
    def __init__(self, nc):
        self.nc = nc
        self.ops = []
        self.lw = {}
        self.rd = {}
        self.last_dma = {}

    def add(self, eng, fn, reads=(), writes=(), kind='c', semkey=None, inc=16, ndma=1):
        o = Op()
        o.eng = eng; o.fn = fn; o.kind = kind; o.semkey = semkey; o.inc = inc; o.ndma = ndma
        o.signal = False; o.ticket = None; o.waits = []; o.idx = len(self.ops); o.sem = None
        raw = set(); oth = set()
        rk = []
        for r in reads:
            rk.extend(r.keys() if isinstance(r, V) else [r])
        wk = []
        for w in writes:
            wk.extend(w.keys() if isinstance(w, V) else [w])
        for k in rk:
            w = self.lw.get(k)
            if w is not None:
                raw.add(w)
        for k in wk:
            w = self.lw.get(k)
            if w is not None:
                oth.add(w)
            d = self.rd.get(k)
            if d:
                oth.update(d.values())
        ekey = eng if kind == 'c' else ('dma', semkey)
        for k in rk:
            self.rd.setdefault(k, {})[ekey] = o.idx
        for k in wk:
            self.lw[k] = o.idx
            self.rd[k] = {}
        if kind != 'c':
            p = self.last_dma.get(semkey)
            if p is not None:
                oth.add(p)
            self.last_dma[semkey] = o.idx
        raw.discard(o.idx); oth.discard(o.idx)
        o.deps_raw = raw
        o.deps_other = oth - raw
        self.ops.append(o)
        return o

    def finalize(self):
        nc = self.nc
        ops = self.ops
        need = [set() for _ in ops]
        for o in ops:
            need[o.idx].update(o.deps_raw)
            for d in o.deps_other:
                p = ops[d]
                if p.kind == 'c' and p.eng == o.eng and o.kind == 'c' and o.eng == 'pe':
                    continue
                need[o.idx].add(d)
        for o in ops:
            for d in need[o.idx]:
                if ops[d].fn is not None:
                    ops[d].signal = True
        cnt = {}
        self.sems = {}
        dcount = {}
        for o in ops:
            if o.fn is None:
                continue
            if o.kind == 'c':
                if o.signal:
                    t = cnt.get(o.eng, 0)
                    cnt[o.eng] = t + 1
                    key = (o.eng, t // SEMW)
                    if key not in self.sems:
                        self.sems[key] = nc.alloc_semaphore(name='s_%s_%d' % key)
                    o.sem = self.sems[key]
                    o.ticket = (key, t % SEMW + 1)
            else:
                key = ('dma', o.semkey)
                if key not in self.sems:
                    self.sems[key] = nc.alloc_semaphore(name='d_%d' % len(self.sems))
                o.sem = self.sems[key]
                c = dcount.get(key, 0) + o.inc * o.ndma
                dcount[key] = c
                o.ticket = (key, c)
        waited = {}
        maxwin = {}
        for o in ops:
            ws = {}
            for d in need[o.idx]:
                p = ops[d]
                if p.fn is None or p.ticket is None:
                    continue
                key, val = p.ticket
                if ws.get(key, 0) < val:
                    ws[key] = val
            wl = []
            for key, val in ws.items():
                wk = (o.eng, key)
                if waited.get(wk, 0) >= val:
                    continue
                if key[0] != 'dma':
                    if maxwin.get((o.eng, key[0]), -1) > key[1]:
                        continue
                    maxwin[(o.eng, key[0])] = max(maxwin.get((o.eng, key[0]), -1), key[1])
                waited[wk] = val
                wl.append((self.sems[key], val))
            o.waits = wl
        self.nsem = len(self.sems)

    def emit(self, block):
        decos = {'pe': block.tensor, 'act': block.scalar, 'dve': block.vector,
                 'pool': block.gpsimd, 'sp': block.sync}
        for ename, deco in decos.items():
            eops = [o for o in self.ops if o.eng == ename]
            if not eops:
                continue

            def body(e, eops=eops):
                for o in eops:
                    waits = list(o.waits)
                    fused = None
                    if FUSE_WAITS and waits and o.fn is not None and o.kind == 'c' and o.eng in ('act', 'dve'):
                        fused = waits.pop()
                    for (sem, val) in waits:
                        e.wait_ge(sem, val)
                    if o.fn is None:
                        continue
                    r = o.fn(e)
                    insts = list(r) if isinstance(r, (list, tuple)) else [r]
                    if fused is not None:
                        insts[0]._wait_ge(fused[0], fused[1])
                    if o.kind != 'c':
                        for i in insts:
                            i.then_inc(o.sem, o.inc)
                    elif o.signal:
                        insts[-1].then_inc(o.sem, 1)
            deco(body)


class Stack:
    def __init__(self, K, lo, hi):
        self.K = K; self.lo = lo; self.hi = hi; self.cur = lo

    def alloc(self, name, shape, dt):
        nbytes = (int(np.prod(shape[1:])) * _esize(dt) + GRAN - 1) // GRAN * GRAN
        self.cur = (self.cur + GRAN - 1) // GRAN * GRAN
        assert self.cur + nbytes <= self.hi, ('SBUF OOM', name, self.cur, nbytes, self.hi)
        self.K.nalloc += 1
        h = self.K.nc.alloc_sbuf_tensor_at('%s_%d' % (name, self.K.nalloc), list(shape), dt, offset=self.cur)
        b = Buf(h, 'sb', self.cur, shape, dt, name)
        self.cur += nbytes
        return b

    def sub(self):
        return Stack(self.K, self.cur, self.hi)


class Banks:
    def __init__(self, bufs):
        self.free = list(bufs)

    def get(self):
        assert self.free, 'PSUM banks exhausted'
        return self.free.pop(0)

    def put(self, b):
        self.free.append(b)


class K:
    def __init__(self, stages):
        self.stages = stages
        nc = self.nc = bass.Bass("TRN2", target_bir_lowering=False)
        self.S = Sched(nc)
        self.nalloc = 0
        lo = (nc._sbuf_addr_for_side('left') + 31) // 32 * 32
        hi = nc._sbuf_addr_for_side('right') // 32 * 32
        self.top = Stack(self, lo, hi)
        self.PS = [Buf(nc.alloc_psum_tensor('ps%d' % i, [128, 512], F32), 'ps', i * 2048, [128, 512], F32, 'ps%d' % i)
                   for i in range(8)]
        self.banks = Banks(self.PS)
        self.din = {}
        self.dout = {}
        self.outkeys = []
        self.nout = 0

    def inp(self, name, shape):
        t = self.nc.dram_tensor(name, list(shape), F32, kind="ExternalInput")
        b = Buf(t, 'dram', 0, shape, F32, name)
        self.din[name] = b
        return b

    def outp(self, name, shape):
        t = self.nc.dram_tensor(name, list(shape), F32, kind="ExternalOutput")
        b = Buf(t, 'dram', 0, shape, F32, name)
        self.dout[name] = b
        return b

    def scratch(self, name, shape, dt):
        t = self.nc.dram_tensor(name, list(shape), dt)
        return Buf(t, 'dram', 0, shape, dt, name)

    def dma(self, q, out, in_, semkey, rk=(), wk=()):
        reads = [in_] + list(rk)
        writes = [out] + list(wk)
        return self.S.add(q, lambda e: e.dma_start(out=out.ap, in_=in_.ap), reads=reads, writes=writes,
                          kind='dma', semkey=semkey)

    def dma_out(self, q, out, in_):
        self.nout += 1
        key = ('out', self.nout)
        self.outkeys.append(key)
        return self.dma(q, out, in_, semkey=('o', self.nout % 8), wk=[key])

    def mm(self, out, lhsT, rhs, start, stop):
        return self.S.add('pe', lambda e: e.matmul(out.ap, lhsT.ap, rhs.ap, start=start, stop=stop),
                          reads=[lhsT, rhs], writes=[out])

    def act(self, out, in_, func, scale=1.0, bias=0.0, accum=None, eng='act'):
        reads = [in_]
        writes = [out]
        kw = {}
        if isinstance(scale, V):
            reads.append(scale); kw['scale'] = scale.ap
        else:
            kw['scale'] = float(scale)
        if isinstance(bias, V):
            reads.append(bias); kw['bias'] = bias.ap
        elif bias != 0.0:
            kw['bias'] = float(bias)
        if accum is not None:
            writes.append(accum); kw['accum_out'] = accum.ap
        return self.S.add('act', lambda e: e.activation(out=out.ap, in_=in_.ap, func=func, **kw),
                          reads=reads, writes=writes)

    def tt(self, out, in0, in1, op, eng='dve'):
        return self.S.add(eng, lambda e: e.tensor_tensor(out=out.ap, in0=in0.ap, in1=in1.ap, op=op),
                          reads=[in0, in1], writes=[out])

    def ts(self, out, in0, s1, s2, op0, op1=None, eng='dve'):
        reads = [in0]
        a1 = s1.ap if isinstance(s1, V) else float(s1)
        if isinstance(s1, V):
            reads.append(s1)
        a2 = None
        if s2 is not None:
            a2 = s2.ap if isinstance(s2, V) else float(s2)
            if isinstance(s2, V):
                reads.append(s2)
        if op1 is None:
            return self.S.add(eng, lambda e: e.tensor_scalar(out=out.ap, in0=in0.ap, scalar1=a1, scalar2=None, op0=op0),
                              reads=reads, writes=[out])
        return self.S.add(eng, lambda e: e.tensor_scalar(out=out.ap, in0=in0.ap, scalar1=a1, scalar2=a2, op0=op0, op1=op1),
                          reads=reads, writes=[out])

    def stt(self, out, in0, scalar, in1, op0, op1, eng='dve'):
        reads = [in0, in1]
        sc = scalar.ap if isinstance(scalar, V) else float(scalar)
        if isinstance(scalar, V):
            reads.append(scalar)
        return self.S.add(eng, lambda e: e.scalar_tensor_tensor(out=out.ap, in0=in0.ap, scalar=sc, in1=in1.ap, op0=op0, op1=op1),
                          reads=reads, writes=[out])

    def copy(self, out, in_, eng='dve'):
        if eng == 'act':
            return self.act(out, in_, AF.Copy)
        return self.S.add(eng, lambda e: e.tensor_copy(out=out.ap, in_=in_.ap), reads=[in_], writes=[out])

    def memset(self, out, val, eng='dve'):
        return self.S.add(eng, lambda e: e.memset(out.ap, val), writes=[out])

    def recip(self, out, in_, eng='dve'):
        return self.S.add(eng, lambda e: e.reciprocal(out=out.ap, in_=in_.ap), reads=[in_], writes=[out])

    def gather(self, groups, src, dst, semkey, rk, wk):
        return self.S.add('pool', lambda e: e.collective_compute("AllGather", ALU.bypass, replica_groups=groups,
                                                                 ins=[src.ap], outs=[dst.ap]),
                          reads=list(rk), writes=list(wk), kind='cc', semkey=semkey, inc=1)

    def stats(self, srcs, n, RS, SQ):
        bank = self.banks.get()
        N = RS.ap.ap[-1][1]
        ps = bank[:, 0:N]
        for i, s in enumerate(srcs):
            sq = SQ[i % 2]
            sqv = sq[:, 0:N]
            self.act(sqv, s, AF.Square)
            self.mm(ps, self.ONES[:, :], sqv, start=(i == 0), stop=(i == len(srcs) - 1))
        self.ts(RS, ps, 1.0 / n, EPS, ALU.mult, ALU.add)
        self.banks.put(bank)
        self.act(RS, RS, AF.Sqrt)
        self.recip(RS, RS)

    def der(self, l, cond, kind, c):
        return self.DER[:, l, cond, kind, c:c + 1]

    def build(self):
        nc = self.nc
        top = self.top
        st = self.stages
        xT = self.inp('xT', [1024, NTOK])
        condT = self.inp('condT', [128, 40])
        sel = self.inp('sel', [128, 4])
        wmodr = self.inp('wmodr', [4, 1024, 768])
        bmodr = self.inp('bmodr', [128, 24])
        ng1 = self.inp('ng1', [128, 32])
        ng2 = self.inp('ng2', [128, 32])
        fg = self.inp('fg', [128, 8])
        psc = self.inp('psc', [128, 16])
        if any(x.startswith('ffn') for x in st):
            self.w_ff1 = self.inp('w_ff1', [4, 1024, 4096])
            self.w_ff2 = self.inp('w_ff2', [4, 4096, 1024])
        self.w_pool = self.inp('w_pool', [2, 4, 256, 256])
        self.pinv = self.inp('pinv', [128, 3 * 4 * 256])
        self.hmask = self.inp('hmask', [128, 2])
        self.w_in = self.inp('w_in', [2, 1024, 1792])
        self.wq = self.inp('wq', [2, 384, 8 * 192])
        self.wk = self.inp('wk', [2, 256, 8 * 128])
        self.wv = self.inp('wv', [2, 256, 8 * 64])
        self.wo = self.inp('wo', [2, 1024, 1024])
        self.wsT = self.inp('wsT', [2, 128, 8 * 128])
        self.bsb = self.inp('bsb', [2, 128, 4 * 128])
        self.vgb = self.inp('vgb', [2, 128, 512])
        self.qag = self.inp('qag', [128, 6])
        self.kvg = self.inp('kvg', [128, 4])
        self.cacheT = self.inp('cacheT', [2, 288, 512])
        self.ropeT = self.inp('ropeT', [2, 64, 2048])
        self.yT = self.outp('yT', [1024, NTOK])
        self.ckv_o = self.outp('ckv_o', [2, 256, 1024])
        self.kpe_o = self.outp('kpe_o', [2, 32, 1024])
        self.modb = self.scratch('modb', [128, 120], F32)
        self.modg = self.scratch('modg', [1024, 120], F32)
        self.hb = self.scratch('hb', [128, 128], F32)
        self.hg = self.scratch('hg', [256, 128], F32)
        self.kvb = [self.scratch('kvb%d' % j, [288, 512], F32) for j in range(4)]
        self.kvgd = [self.scratch('kvgd%d' % j, [576, 512], F32) for j in range(4)]
        self.qnb = self.scratch('qnb', [384, 2048], BF16)

        self.X = top.alloc('X', [128, 8, NTOK], F32)
        self.ONES = top.alloc('ONES', [128, 128], BF16)
        self.ONESF = top.alloc('ONESF', [128, 64], F32)
        self.DER = top.alloc('DER', [128, 4, 2, 6, 8], F32)
        self.FG = top.alloc('FG', [128, 8], F32)
        self.QAG = top.alloc('QAG', [128, 6], F32)
        self.KVG = top.alloc('KVG', [128, 4], F32)
        self.HM = top.alloc('HM', [128, 2], F32)
        X = self.X
        self.memset(self.ONES[:, :], 1.0)
        self.memset(self.ONESF[:, :], 1.0)
        for c in range(8):
            self.dma('sp', X[:, c, :], xT[c * 128:(c + 1) * 128, :], semkey=('X', c))
        self.dma('sp', self.FG[:, :], fg[:, :], semkey='small')
        self.dma('sp', self.QAG[:, :], self.qag[:, :], semkey='small')
        self.dma('sp', self.KVG[:, :], self.kvg[:, :], semkey='small')
        self.dma('sp', self.HM[:, :], self.hmask[:, :], semkey='small')
        self.setup_mod(top.sub(), condT, sel, wmodr, bmodr, ng1, ng2, psc)

        for l in range(4):
            if ('mix%d' % l) in st:
                if l % 2 == 0:
                    self.even_layer(l, top.sub())
                else:
                    self.odd_layer(l, top.sub())
            if ('ffn%d' % l) in st:
                self.ffn(l, top.sub())
        self.final(top.sub())
        self.S.add('sp', None, reads=list(self.outkeys))
        self.S.finalize()
        with nc.Block() as block:
            self.S.emit(block)
        return nc

    def setup_mod(self, stk, condT, sel, wmodr, bmodr, ng1, ng2, psc):
        CT = stk.alloc('CT', [128, 40], F32)
        ST = stk.alloc('ST', [128, 8, 5], BF16)
        SEL = stk.alloc('SEL', [128, 4], F32)
        BM = stk.alloc('BM', [128, 24], F32)
        NG1 = stk.alloc('NG1', [128, 4, 8], F32)
        NG2 = stk.alloc('NG2', [128, 4, 8], F32)
        PSC = stk.alloc('PSC', [128, 2, 8], F32)
        MODP = stk.alloc('MODP', [128, 120], F32)
        MODALL = stk.alloc('MODALL', [128, 8, 120], F32)
        MS = stk.alloc('MS', [128, 8, 24, 2], F32)
        WM = [stk.alloc('WM%d' % l, [128, 8, 768], BF16) for l in range(4)]
        self.dma('sp', CT[:, :], condT[:, :], semkey='small')
        self.dma('sp', SEL[:, :], sel[:, :], semkey='small')
        self.dma('sp', BM[:, :], bmodr[:, :], semkey='small')
        self.dma('sp', NG1[:, :, :], ng1.ap(0, [[32, 128], [8, 4], [1, 8]]), semkey='small')
        self.dma('sp', NG2[:, :, :], ng2.ap(0, [[32, 128], [8, 4], [1, 8]]), semkey='small')
        self.dma('sp', PSC[:, :, :], psc.ap(0, [[16, 128], [8, 2], [1, 8]]), semkey='small')
        for l in range(4):
            self.dma('pool', WM[l][:, :, :], wmodr.ap(l * 1024 * 768, [[768, 128], [128 * 768, 8], [1, 768]]),
                     semkey=('wm', l))
        self.act(ST.ap(0, [[40, 128], [1, 40]]), CT[:, :], AF.Silu)
        bank = self.banks.get()
        for lm in range(24):
            l, m = lm // 6, lm % 6
            for k in range(8):
                self.mm(bank[:, lm * 5:(lm + 1) * 5], WM[l][:, k, m * 128:(m + 1) * 128], ST[:, k, :],
                        start=(k == 0), stop=(k == 7))
        self.tt(MODP.ap(0, [[120, 128], [5, 24], [1, 5]]), bank.ap(0, [[512, 128], [5, 24], [1, 5]]),
                BM.ap(0, [[24, 128], [1, 24], [0, 5]]), ALU.add)
        self.banks.put(bank)
        self.dma('sp', self.modb[:, :], MODP[:, :], semkey='modb', wk=[('dram', 'modb')])
        self.gather([list(range(8))], self.modb[:, :], self.modg[:, :], semkey='cc_mod',
                    rk=[('dram', 'modb')], wk=[('dram', 'modg')])
        self.dma('sp', MODALL[:, :, :], self.modg.ap(0, [[120, 128], [128 * 120, 8], [1, 120]]), semkey='modall',
                 rk=[('dram', 'modg')])
        self.copy(MS.ap(0, [[384, 128], [48, 8], [2, 24]]), MODALL.ap(0, [[960, 128], [120, 8], [5, 24]]))
        ms1 = MS.ap(1, [[384, 128], [48, 8], [2, 24]])
        self.ts(ms1, MODALL.ap(1, [[960, 128], [120, 8], [5, 24]]), SEL[:, 0:1], None, ALU.mult)
        for j in range(1, 4):
            self.stt(ms1, MODALL.ap(1 + j, [[960, 128], [120, 8], [5, 24]]), SEL[:, j:j + 1], ms1, ALU.mult, ALU.add)
        for l in range(4):
            for cond in range(2):
                def msv(m):
                    return MS.ap((l * 6 + m) * 2 + cond, [[384, 128], [48, 8]])
                d = lambda kind: self.DER[:, l, cond, kind, :]
                self.stt(d(0), msv(1), 1.0, NG1[:, l, :], ALU.add, ALU.mult)
                self.copy(d(1), msv(0))
                if l % 2 == 1:
                    self.tt(d(2), msv(2), PSC[:, l // 2, :], ALU.mult)
                else:
                    self.copy(d(2), msv(2))
                self.stt(d(3), msv(4), 1.0, NG2[:, l, :], ALU.add, ALU.mult)
                self.copy(d(4), msv(3))
                self.copy(d(5), msv(5))

    def final(self, stk):
        X = self.X
        SQ = [stk.alloc('SQ%d' % i, [128, 512], BF16) for i in range(2)]
        RS = stk.alloc('RS', [128, 512], F32)
        YS = [stk.alloc('YS%d' % i, [128, 512], F32) for i in range(3)]
        n = 0
        for t in range(NT):
            tsl = slice(t * 512, (t + 1) * 512)
            self.stats([X[:, c, tsl] for c in range(8)], 1024, RS[:, :], SQ)
            for c in range(8):
                ys = YS[n % 3]; n += 1
                self.stt(ys[:, :], X[:, c, tsl], self.FG[:, c:c + 1], RS[:, :], ALU.mult, ALU.mult)
                self.dma_out('sp', self.yT[c * 128:(c + 1) * 128, tsl], ys[:, :])

    def ffn(self, l, stk):
        X = self.X
        H = stk.alloc('H', [128, 8, NTOK], BF16)
        W1 = [stk.alloc('W1_%d' % i, [128, 8, 512], BF16) for i in range(2)]
        W2 = [stk.alloc('W2_%d' % i, [128, 4, 1024], BF16) for i in range(2)]
        A = [stk.alloc('A%d' % i, [128, 4, 512], BF16) for i in range(2)]
        SQ = [stk.alloc('SQ%d' % i, [128, 512], BF16) for i in range(2)]
        RS = stk.alloc('RS', [128, 512], F32)
        TMP = [stk.alloc('TMP%d' % i, [128, 512], F32) for i in range(2)]

        def load(j):
            b = j % 2
            self.dma('pool', W1[b][:, :, :],
                     self.w_ff1.ap(l * 1024 * 4096 + j * 512, [[4096, 128], [128 * 4096, 8], [1, 512]]),
                     semkey=('w1', b))
            self.dma('pool', W2[b][:, :, :],
                     self.w_ff2.ap(l * 4096 * 1024 + j * 512 * 1024, [[1024, 128], [128 * 1024, 4], [1, 1024]]),
                     semkey=('w2', b))
        load(0)
        for t in range(NT):
            cond = 0 if t < 2 else 1
            tsl = slice(t * 512, (t + 1) * 512)
            self.stats([X[:, c, tsl] for c in range(8)], 1024, RS[:, :], SQ)
            for c in range(8):
                tmp = TMP[c % 2]
                self.tt(tmp[:, :], X[:, c, tsl], RS[:, :], ALU.mult)
                self.act(H[:, c, tsl], tmp[:, :], AF.Identity, scale=self.der(l, cond, 3, c), bias=self.der(l, cond, 4, c))
        na = 0
        for j in range(8):
            if j + 1 < 8:
                load(j + 1)
            b = j % 2
            for t in range(NT):
                cond = 0 if t < 2 else 1
                tsl = slice(t * 512, (t + 1) * 512)
                a = A[na % 2]; na += 1
                for fc in range(4):
                    bank = self.banks.get()
                    for k in range(8):
                        self.mm(bank[:, :], W1[b][:, k, fc * 128:(fc + 1) * 128], H[:, k, tsl], start=(k == 0), stop=(k == 7))
                    self.act(a[:, fc, :], bank[:, :], AF.Relu)
                    self.banks.put(bank)
                    self.tt(a[:, fc, :], a[:, fc, :], a[:, fc, :], ALU.mult)
                for oc in range(8):
                    bank = self.banks.get()
                    for fc in range(4):
                        self.mm(bank[:, :], W2[b][:, fc, oc * 128:(oc + 1) * 128], a[:, fc, :], start=(fc == 0), stop=(fc == 3))
                    self.stt(X[:, oc, tsl], bank[:, :], self.der(l, cond, 5, oc), X[:, oc, tsl], ALU.mult, ALU.add)
                    self.banks.put(bank)

    def odd_layer(self, l, stk):
        X = self.X
        jj = l // 2
        RSA = stk.alloc('RSA', [128, NTOK], F32)
        SQ = [stk.alloc('SQ%d' % i, [128, 512], BF16) for i in range(2)]
        HP = [stk.alloc('HP%d' % i, [128, 8, 272], F32) for i in range(2)]
        WP = stk.alloc('WP', [128, 4, 2, 256], BF16)
        PINV = stk.alloc('PINV', [128, 3, 4, 256], F32)
        TMP = [stk.alloc('TMP%d' % i, [128, 272], F32) for i in range(2)]
        SA = stk.alloc('SA', [128, 272], F32)
        SB = stk.alloc('SB', [128, 272], F32)
        DT = stk.alloc('DT', [128, 256], F32)
        DD = [stk.alloc('DD%d' % i, [128, 2, 256], BF16) for i in range(2)]
        T16 = stk.alloc('T16', [128, 16], F32)
        HB = stk.alloc('HB', [128, 8, 16], F32)
        HG = stk.alloc('HG', [128, 2, 8, 16], F32)
        LH = stk.alloc('LH', [128, 8, 8], F32)
        RH = stk.alloc('RH', [128, 8, 8], F32)
        self.dma('pool', WP[:, :, :, :], self.w_pool.ap(jj * 4 * 256 * 256, [[256, 128], [256 * 256, 4], [128 * 256, 2], [1, 256]]),
                 semkey='wp')
        self.dma('sp', PINV.ap(0, [[3072, 128], [1, 3072]]), self.pinv[:, :], semkey='pinv')
        for t in range(NT):
            tsl = slice(t * 512, (t + 1) * 512)
            self.stats([X[:, c, tsl] for c in range(8)], 1024, RSA[:, tsl], SQ)
        for c in range(8):
            self.tt(T16.ap(0, [[16, 128], [8, 2], [1, 8]]), X.ap(c * NTOK + 1024, [[8 * NTOK, 128], [2040, 2], [1, 8]]),
                    RSA.ap(1024, [[NTOK, 128], [2040, 2], [1, 8]]), ALU.mult)
            self.act(HB[:, c, :], T16[:, :], AF.Identity, scale=self.der(l, 1, 0, c), bias=self.der(l, 1, 1, c))
        self.dma('sp', self.hb[:, :], HB.ap(0, [[128, 128], [1, 128]]), semkey='hb', wk=[('dram', 'hb')])
        self.gather([[0, 1], [2, 3], [4, 5], [6, 7]], self.hb[:, :], self.hg[:, :], semkey='cc_h',
                    rk=[('dram', 'hb')], wk=[('dram', 'hg')])
        self.dma('sp', HG.ap(0, [[256, 128], [128, 2], [1, 128]]), self.hg.ap(0, [[128, 128], [128 * 128, 2], [1, 128]]),
                 semkey='hg', rk=[('dram', 'hg')])
        self.ts(LH[:, :, :], HG[:, 0, :, 8:16], self.HM[:, 0:1], None, ALU.mult)
        self.ts(RH[:, :, :], HG[:, 1, :, 0:8], self.HM[:, 1:2], None, ALU.mult)
        segs = [(False, s, s * 256) for s in range(4)] + [(True, s, 1024 + s * 256) for s in range(8)]
        ndd = 0
        for n, (smp, s, tok0) in enumerate(segs):
            hp = HP[n % 2]
            hprev = HP[(n - 1) % 2]
            cond = 1 if smp else 0
            ncol = 264 if (smp and s < 7) else 256
            for c in range(8):
                tmp = TMP[c % 2]
                self.tt(tmp[:, 0:ncol], X[:, c, tok0:tok0 + ncol], RSA[:, tok0:tok0 + ncol], ALU.mult)
                self.act(hp[:, c, 8:8 + ncol], tmp[:, 0:ncol], AF.Identity, scale=self.der(l, cond, 0, c), bias=self.der(l, cond, 1, c))
            if not smp:
                self.memset(hp[:, :, 0:8], 0.0)
                self.memset(hp[:, :, 264:272], 0.0)
                kind = 0
            else:
                if s == 0:
                    self.copy(hp[:, :, 0:8], LH[:, :, :])
                else:
                    self.copy(hp[:, :, 0:8], hprev[:, :, 256:264])
                if s == 7:
                    self.copy(hp[:, :, 264:272], RH[:, :, :])
                kind = 1 if s == 0 else (2 if s == 7 else None)
            for gi, w in enumerate((2, 4, 8, 16)):
                dd = DD[ndd % 2]; ndd += 1
                for kc in range(2):
                    c = 2 * gi + kc
                    hv = hp[:, c, :]
                    hvb = lambda a, b: hp[:, c, a:b]
                    self.tt(SA[:, 1:272], hvb(0, 271), hvb(1, 272), ALU.add)
                    Fb = SA
                    if w >= 4:
                        self.tt(SB[:, 2:271], SA[:, 1:270], SA[:, 3:272], ALU.add); Fb = SB
                    if w >= 8:
                        self.tt(SA[:, 4:269], SB[:, 2:267], SB[:, 6:271], ALU.add); Fb = SA
                    if w >= 16:
                        self.tt(SB[:, 8:265], SA[:, 4:261], SA[:, 12:269], ALU.add); Fb = SB
                    if kind is None:
                        self.stt(dd[:, kc, :], Fb[:, 8:264], 1.0 / w, hvb(8, 264), ALU.mult, ALU.subtract)
                    else:
                        self.tt(DT[:, :], Fb[:, 8:264], PINV[:, kind, gi, :], ALU.mult)
                        self.tt(dd[:, kc, :], DT[:, :], hvb(8, 264), ALU.subtract)
                for oc in range(2):
                    co = 2 * gi + oc
                    bank = self.banks.get()
                    for kc in range(2):
                        self.mm(bank[:, 0:256], WP[:, gi, kc, oc * 128:(oc + 1) * 128], dd[:, kc, :], start=(kc == 0), stop=(kc == 1))
                    self.stt(X[:, co, tok0:tok0 + 256], bank[:, 0:256], self.der(l, cond, 2, co), X[:, co, tok0:tok0 + 256], ALU.mult, ALU.add)
                    self.banks.put(bank)

    def attention(self, l, i, stk, L, CKV, KPE, qtiles, WK, WV, WQh, WOAh):
        X = self.X
        nqmax = max(q[1] for q in qtiles)
        nch = L // 128
        KH = [stk.alloc('KH%d' % j, [128, L], BF16) for j in range(2)]
        VH = [stk.alloc('VH%d' % j, [128, nch, 72], BF16) for j in range(2)]
        QH = [stk.alloc('QH%d' % j, [128, nqmax], BF16) for j in range(2)]
        LA = 3 if nch > 3 else 1
        NPT = LA + 2
        PT = [stk.alloc('PT%d' % j, [128, nqmax], BF16) for j in range(NPT)]
        OS = stk.alloc('OS', [65, nqmax], F32)
        RC = stk.alloc('RC', [65, nqmax], F32)
        AT = [stk.alloc('AT%d' % j, [64, nqmax], BF16) for j in range(2)]
        rope_any = any(q[3] is not None for q in qtiles)
        if rope_any:
            T1 = stk.alloc('T1', [64, nqmax], F32)
            T2 = stk.alloc('T2', [64, nqmax], F32)
        for j in range(2):
            self.memset(KH[j][0:32, :], 0.0)
            self.memset(QH[j][0:32, :], 0.0)
            self.memset(VH[j][:, :, 64:65], 1.0)
        nqh = 0
        ncp = 0
        for h in range(8):
            wq = WQh[h % 2]
            woa = WOAh[h % 2]
            self.dma('pool', wq[:, :, :], self.wq.ap(i * 384 * 1536 + h * 192, [[1536, 128], [128 * 1536, 3], [1, 192]]),
                     semkey=('wq', h % 2))
            self.dma('pool', woa[0:64, :], self.wo.ap(i * 1024 * 1024 + h * 64 * 1024, [[1024, 64], [1, 1024]]),
                     semkey=('woa', h % 2))
            kh = KH[h % 2]
            vh = VH[h % 2]
            for kt0 in range(0, L, 512):
                n = min(512, L - kt0)
                bank = self.banks.get()
                for kc in range(2):
                    self.mm(bank[:, 0:n], WK[:, kc, h * 128:(h + 1) * 128], CKV[:, kc, kt0:kt0 + n], start=(kc == 0), stop=(kc == 1))
                self.copy(kh[64:128, kt0:kt0 + n], bank[64:128, 0:n], eng='act')
                self.banks.put(bank)
                self.copy(kh[32:64, kt0:kt0 + n], KPE[32:64, kt0:kt0 + n], eng='dve')
            for c0 in range(0, nch, 8):
                nb = min(8, nch - c0)
                bank = self.banks.get()
                for ci in range(nb):
                    for kc in range(2):
                        self.mm(bank[:, ci * 64:(ci + 1) * 64], CKV[:, kc, (c0 + ci) * 128:(c0 + ci + 1) * 128],
                                WV[:, kc, h * 64:(h + 1) * 64], start=(kc == 0), stop=(kc == 1))
                self.copy(vh[:, c0:c0 + nb, 0:64], bank.ap(0, [[512, 128], [64, nb], [1, 64]]), eng=('act' if ncp % 2 else 'dve'))
                ncp += 1
                self.banks.put(bank)
            for (qnf, nq, xtok0, rope, cond) in qtiles:
                qh = QH[nqh % 2]
                at = AT[nqh % 2]
                nqh += 1
                bankA = self.banks.get()
                for kc in range(3):
                    self.mm(bankA[:, 0:nq], wq[:, kc, 0:128], qnf(kc), start=(kc == 0), stop=(kc == 2))
                self.copy(qh[64:128, 0:nq], bankA[64:128, 0:nq], eng='act')
                if rope is not None:
                    bankB = self.banks.get()
                    for kc in range(3):
                        self.mm(bankB[0:64, 0:nq], wq[:, kc, 128:192], qnf(kc), start=(kc == 0), stop=(kc == 2))
                    self.tt(T1[32:64, 0:nq], bankA[32:64, 0:nq], rope[0], ALU.mult)
                    self.tt(T2[32:64, 0:nq], bankB[32:64, 0:nq], rope[1], ALU.mult)
                    self.banks.put(bankB)
                    self.tt(qh[32:64, 0:nq], T1[32:64, 0:nq], T2[32:64, 0:nq], ALU.add)
                else:
                    self.copy(qh[32:64, 0:nq], bankA[32:64, 0:nq], eng='act')
                self.banks.put(bankA)
                obank = self.banks.get()
                sbs = {}

                def issue_s(kc):
                    sb = self.banks.get()
                    sbs[kc] = sb
                    self.mm(sb[:, 0:nq], kh[:, kc * 128:(kc + 1) * 128], qh[:, 0:nq], start=True, stop=True)
                for kk in range(min(LA, nch)):
                    issue_s(kk)
                for kc in range(nch):
                    if kc + LA < nch:
                        issue_s(kc + LA)
                    pt = PT[kc % NPT]
                    sb = sbs.pop(kc)
                    self.act(pt[:, 0:nq], sb[:, 0:nq], AF.Exp, scale=ATT_SCALE)
                    self.banks.put(sb)
                    self.mm(obank[0:65, 0:nq], vh[:, kc, 0:65], pt[:, 0:nq], start=(kc == 0), stop=(kc == nch - 1))
                self.copy(OS[0:65, 0:nq], obank[0:65, 0:nq], eng='act')
                self.banks.put(obank)
                self.recip(RC[64:65, 0:nq], OS[64:65, 0:nq])
                bb = self.banks.get()
                self.mm(bb[0:64, 0:nq], self.ONESF[64:65, 0:64], RC[64:65, 0:nq], start=True, stop=True)
                self.tt(at[0:64, 0:nq], OS[0:64, 0:nq], bb[0:64, 0:nq], ALU.mult)
                self.banks.put(bb)
                for oc in range(8):
                    wb = self.banks.get()
                    self.mm(wb[:, 0:nq], woa[0:64, oc * 128:(oc + 1) * 128], at[0:64, 0:nq], start=True, stop=True)
                    xs = X[:, oc, xtok0:xtok0 + nq]
                    self.stt(xs, wb[:, 0:nq], self.der(l, cond, 2, oc), xs, ALU.mult, ALU.add)
                    self.banks.put(wb)

    def even_layer(self, l, stk):
        X = self.X
        i = l // 2
        WK = stk.alloc('WK', [128, 2, 1024], BF16)
        WV = stk.alloc('WV', [128, 2, 512], BF16)
        WQh = [stk.alloc('WQh%d' % j, [128, 3, 192], BF16) for j in range(2)]
        WOAh = [stk.alloc('WOAh%d' % j, [64, 1024], BF16) for j in range(2)]
        ROPE = stk.alloc('ROPE', [64, 2, 2048], BF16)
        self.dma('pool', WK[:, :, :], self.wk.ap(i * 256 * 1024, [[1024, 128], [128 * 1024, 2], [1, 1024]]), semkey='wk')
        self.dma('pool', WV[:, :, :], self.wv.ap(i * 256 * 512, [[512, 128], [128 * 512, 2], [1, 512]]), semkey='wv')

        def load_rope():
            for j in range(2):
                self.dma('pool', ROPE[32:64, j, :], self.ropeT[j, 32:64, :], semkey='rope')
        load_rope()
        e1 = stk.sub()
        WIN = e1.alloc('WIN', [128, 8, 1792], BF16)
        WOG = e1.alloc('WOG', [128, 4, 1024], BF16)
        WST = e1.alloc('WST', [128, 8, 128], BF16)
        BSB = e1.alloc('BSB', [128, 4, 128], F32)
        VGB = e1.alloc('VGB', [128, 512], F32)
        SQ = [e1.alloc('SQ%d' % j, [128, 512], BF16) for j in range(2)]
        RS = e1.alloc('RS', [128, 512], F32)
        TMP = [e1.alloc('TMP%d' % j, [128, 512], F32) for j in range(2)]
        H = e1.alloc('H', [128, 8, 512], BF16)
        RSQ = e1.alloc('RSQ', [128, 512], F32)
        QNS = e1.alloc('QNS', [128, 3, 512], BF16)
        C32 = [e1.alloc('C32_%d' % j, [128, 512], F32) for j in range(2)]
        CKV16 = e1.alloc('CKV16', [128, 2, 512], BF16)
        KPE16 = e1.alloc('KPE16', [64, 512], BF16)
        U = e1.alloc('U', [128, 4, 512], BF16)
        GV = e1.alloc('GV', [128, 512], F32)
        TS_ = e1.alloc('TS', [128, 512], F32)
        SSV = e1.alloc('SSV', [128, 8], F32)
        VNP = [e1.alloc('VNP%d' % j, [128, 8, 128], BF16) for j in range(2)]
        self.dma('pool', WIN[:, :, :], self.w_in.ap(i * 1024 * 1792, [[1792, 128], [128 * 1792, 8], [1, 1792]]), semkey='win')
        self.dma('pool', WOG[:, :, :], self.wo.ap(i * 1024 * 1024 + 512 * 1024, [[1024, 128], [128 * 1024, 4], [1, 1024]]), semkey='wog')
        self.dma('pool', WST.ap(0, [[1024, 128], [1, 1024]]), self.wsT[i, :, :], semkey='wst')
        self.dma('sp', BSB.ap(0, [[512, 128], [1, 512]]), self.bsb[i, :, :], semkey='bsb')
        self.dma('sp', VGB[:, :], self.vgb[i, :, :], semkey='vgb')
        for j in range(2):
            self.memset(VNP[j][:, :, :], 0.0)
        n32 = [0]

        def e1_tile(t):
            cond = 0 if t < 2 else 1
            smp = t >= 2
            tsl = slice(t * 512, (t + 1) * 512)
            self.stats([X[:, c, tsl] for c in range(8)], 1024, RS[:, :], SQ)
            for c in range(8):
                tmp = TMP[c % 2]
                self.tt(tmp[:, :], X[:, c, tsl], RS[:, :], ALU.mult)
                self.act(H[:, c, :], tmp[:, :], AF.Identity, scale=self.der(l, cond, 0, c), bias=self.der(l, cond, 1, c))

            def gelu(out, ps, tmp):
                self.act(tmp, ps, AF.Square, scale=0.21145921592861536)
                self.stt(tmp, tmp, 1.0, ps, ALU.add, ALU.mult)
                self.act(tmp, tmp, AF.Exp, scale=-1.5957691216057308)
                self.ts(tmp, tmp, 1.0, None, ALU.add)
                self.recip(tmp, tmp)
                self.tt(out, tmp, ps, ALU.mult)

            def proj(col0, m, bank_rows=None):
                bank = self.banks.get()
                for k in range(8):
                    self.mm(bank[0:m, :], WIN[:, k, col0:col0 + m], H[:, k, :], start=(k == 0), stop=(k == 7))
                return bank
            if DBG_E1 < 2:
                return
            qb = [proj(c * 128, 128) for c in range(3)]
            self.stats([b[:, :] for b in qb], 384, RSQ[:, :], SQ)
            for c in range(3):
                self.stt(QNS[:, c, :], qb[c][:, :], self.QAG[:, i * 3 + c:i * 3 + c + 1], RSQ[:, :], ALU.mult, ALU.mult)
                self.banks.put(qb[c])
            if smp:
                for c in range(3):
                    self.dma('sp', self.qnb[c * 128:(c + 1) * 128, (t - 2) * 512:(t - 1) * 512], QNS[:, c, :], semkey='qnb',
                             wk=[('dram', 'qnb', t)])
            if DBG_E1 < 3:
                return
            kb = [proj(384 + c * 128, 128) for c in range(2)]
            self.stats([b[:, :] for b in kb], 256, RSQ[:, :], SQ)
            for c in range(2):
                c32 = C32[n32[0] % 2]; n32[0] += 1
                self.stt(c32[:, :], kb[c][:, :], self.KVG[:, i * 2 + c:i * 2 + c + 1], RSQ[:, :], ALU.mult, ALU.mult)
                self.banks.put(kb[c])
                self.copy(CKV16[:, c, :], c32[:, :], eng='act')
                if smp:
                    self.dma('sp', self.kvb[t - 2][c * 128:(c + 1) * 128, :], c32[:, :], semkey='kvb',
                             wk=[('dram', 'kvb', t, c)])
                else:
                    self.dma_out('sp', self.ckv_o[i, c * 128:(c + 1) * 128, t * 512:(t + 1) * 512], c32[:, :])
            if DBG_E1 < 4:
                return
            pb = proj(1664, 64)
            if smp:
                pb2 = proj(1728, 64)
                tcol = slice((t - 2) * 512, (t - 1) * 512)
                self.tt(GV[32:64, :], pb[32:64, :], ROPE[32:64, 0, tcol], ALU.mult)
                self.tt(TS_[32:64, :], pb2[32:64, :], ROPE[32:64, 1, tcol], ALU.mult)
                self.banks.put(pb); self.banks.put(pb2)
                self.tt(GV[32:64, :], GV[32:64, :], TS_[32:64, :], ALU.add)
                self.dma('sp', self.kvb[t - 2][256:288, :], GV[32:64, :], semkey='kvb',
                         wk=[('dram', 'kvb', t, 2)])
                self.gather([[0, 1], [2, 3], [4, 5], [6, 7]], self.kvb[t - 2][:, :], self.kvgd[t - 2][:, :], semkey='cc_kv',
                            rk=[('dram', 'kvb', t, c) for c in range(3)], wk=[('dram', 'kvgd', t)])
            else:
                self.copy(GV[32:64, :], pb[32:64, :], eng='act')
                self.banks.put(pb)
                self.copy(KPE16[32:64, :], GV[32:64, :], eng='dve')
                self.dma_out('sp', self.kpe_o[i, :, t * 512:(t + 1) * 512], GV[32:64, :])
            if DBG_E1 < 5:
                return
            for c in range(4):
                ub = proj(640 + c * 128, 128)
                gelu(U[:, c, :], ub[:, :], TMP[c % 2][:, :])
                self.banks.put(ub)
            if DBG_E1 < 6:
                return
            sbanks = [self.banks.get() for _ in range(4)]
            for sc in range(4):
                vb = self.banks.get()
                for k in range(8):
                    self.mm(vb[:, :], H[:, k, sc * 128:(sc + 1) * 128], WIN[:, k, 1152:1664], start=(k == 0), stop=(k == 7))
                gelu(GV[:, :], vb[:, :], TMP[sc % 2][:, :])
                self.banks.put(vb)
                ssv = SSV[:, sc:sc + 1]
                self.act(TS_[:, :], GV[:, :], AF.Square, accum=ssv)
                self.ts(ssv, ssv, 1.0 / 512, EPS, ALU.mult, ALU.add)
                self.act(ssv, ssv, AF.Sqrt)
                self.recip(ssv, ssv)
                vnp = VNP[sc % 2]
                for par in range(2):
                    self.stt(vnp.ap(par * 192, [[1024, 128], [256, 4], [1, 64]]), GV.ap(par * 64, [[512, 128], [128, 4], [1, 64]]),
                             ssv, VGB.ap(par * 64, [[512, 128], [128, 4], [1, 64]]), ALU.mult, ALU.mult)
                for gp in range(4):
                    for par in range(2):
                        g = 2 * gp + par
                        self.mm(sbanks[gp][:, sc * 128:(sc + 1) * 128], vnp[:, g, :], WST[:, g, :], start=(par == 0), stop=(par == 1))
            for gp in range(4):
                self.tt(TS_.ap(0, [[512, 128], [128, 4], [1, 128]]), sbanks[gp].ap(0, [[512, 128], [128, 4], [1, 128]]),
                        BSB.ap(gp * 128, [[512, 128], [0, 4], [1, 128]]), ALU.add)
                self.banks.put(sbanks[gp])
                self.tt(U[:, gp, :], TS_[:, :], U[:, gp, :], ALU.mult)
            if DBG_E1 < 7:
                return
            for oc in range(8):
                wb = self.banks.get()
                for kc in range(4):
                    self.mm(wb[:, :], WOG[:, kc, oc * 128:(oc + 1) * 128], U[:, kc, :], start=(kc == 0), stop=(kc == 3))
                self.stt(X[:, oc, tsl], wb[:, :], self.der(l, cond, 2, oc), X[:, oc, tsl], ALU.mult, ALU.add)
                self.banks.put(wb)

        for t in range(2, 6):
            e1_tile(t)
        if DBG_EVEN < 2:
            return
        pa = Stack(self, ROPE.base, ROPE.base + 8192)
        for t in range(2 if DBG_EVEN >= 3 else 0):
            e1_tile(t)
            for sq in range(2):
                o = sq * 256
                CKVp = Buf(CKV16.h, 'sb', CKV16.base, CKV16.shape, BF16, 'CKVp')
                ckv_v = lambda kcs, a, b, o=o: None
                qt = [((lambda kc, o=o: QNS[:, kc, o:o + 256]), 256, t * 512 + o, None, 0)]
                self.attention_ctx(l, i, Stack(self, pa.lo, pa.hi), 256, CKV16, KPE16, o, qt, WK, WV, WQh, WOAh)
        sa = stk.sub()
        CKV = sa.alloc('CKV', [128, 2, 4608], BF16)
        KPE = sa.alloc('KPE', [64, 4608], BF16)
        QN = sa.alloc('QN', [128, 3, 2048], BF16)
        load_rope()
        for kc in range(2):
            self.dma('pool', CKV[:, kc, 0:512], self.cacheT[i, kc * 128:(kc + 1) * 128, :], semkey='ckvl')
            for r in range(2):
                for j in range(4):
                    k0 = 512 + r * 2048 + j * 512
                    self.dma('pool', CKV[:, kc, k0:k0 + 512], self.kvgd[j][r * 288 + kc * 128:r * 288 + (kc + 1) * 128, :],
                             semkey='ckvl2', rk=[('dram', 'kvgd', j + 2)])
        self.dma('pool', KPE[32:64, 0:512], self.cacheT[i, 256:288, :], semkey='ckvl')
        for r in range(2):
            for j in range(4):
                k0 = 512 + r * 2048 + j * 512
                self.dma('pool', KPE[32:64, k0:k0 + 512], self.kvgd[j][r * 288 + 256:r * 288 + 288, :],
                         semkey='ckvl2', rk=[('dram', 'kvgd', j + 2)])
        for kc in range(3):
            self.dma('sp', QN[:, kc, :], self.qnb[kc * 128:(kc + 1) * 128, :], semkey='qnl',
                     rk=[('dram', 'qnb', t) for t in range(2, 6)])
        qts = []
        for qi in range(4):
            qts.append(((lambda kc, qi=qi: QN[:, kc, qi * 512:(qi + 1) * 512]), 512, 1024 + qi * 512,
                        (ROPE[32:64, 0, qi * 512:(qi + 1) * 512], ROPE[32:64, 1, qi * 512:(qi + 1) * 512]), 1))
        if DBG_EVEN >= 4:
            self.attention_ctx(l, i, sa, 4608, CKV, KPE, 0, qts, WK, WV, WQh, WOAh)

    def attention_ctx(self, l, i, stk, L, CKV, KPE, koff, qtiles, WK, WV, WQh, WOAh):
        class Off:
            def __init__(s, b, off, three):
                s.b = b; s.off = off; s.three = three

            def __getitem__(s, idx):
                if s.three:
                    p, kc, cs = idx
                    return s.b[p, kc, slice(cs.start + s.off, cs.stop + s.off)]
                p, cs = idx
                return s.b[p, slice(cs.start + s.off, cs.stop + s.off)]
        self.attention(l, i, stk, L, Off(CKV, koff, True), Off(KPE, koff, False), qtiles, WK, WV, WQh, WOAh)


def build_nc(stages):
    k = K(stages)
    nc = k.build()
    return nc, k


ALL_STAGES = ('mix0', 'ffn0', 'mix1', 'ffn1', 'mix2', 'ffn2', 'mix3', 'ffn3')
_CACHE = {}


def prep_inputs(inp):
    f = lambda a: np.ascontiguousarray(a, dtype=np.float32)
    x_prompt = np.asarray(inp['x_prompt']); x_sample = np.asarray(inp['x_sample'])
    c = np.asarray(inp['c']); c_ctx = np.asarray(inp['c_ctx'])
    w_mod = np.asarray(inp['w_mod']); b_mod = np.asarray(inp['b_mod'])
    C5 = np.concatenate([c_ctx[None, :], c], axis=0)
    condT = f(C5.reshape(5, 8, 128).transpose(2, 1, 0).reshape(128, 40))
    fm = lambda v: v.reshape(-1, 128).T
    ng1 = f(np.stack([fm(np.asarray(inp['norm1_g'])[l]) for l in range(4)], 1).reshape(128, 32))
    ng2 = f(np.stack([fm(np.asarray(inp['norm2_g'])[l]) for l in range(4)], 1).reshape(128, 32))
    fg = f(fm(np.asarray(inp['final_g'])))
    psc = f(np.stack([fm(np.asarray(inp['pool_scale'])[j]) for j in range(2)], 1).reshape(128, 16))
    qag = f(np.stack([fm(np.asarray(inp['q_a_g'])[i]) for i in range(2)], 1).reshape(128, 6))
    kvg = f(np.stack([fm(np.asarray(inp['kv_a_g'])[i]) for i in range(2)], 1).reshape(128, 4))
    w_in_ab = np.asarray(inp['w_in_ab'])
    d = np.arange(32)
    swap = (d // 16) * 16 + (1 - (d % 16) // 8) * 8 + d % 8
    z32 = np.zeros((2, 1024, 32), np.float32)
    kpe = w_in_ab[:, :, 640:672]
    w_in = f(np.concatenate([w_in_ab[:, :, 0:640], w_in_ab[:, :, 672:1696], z32, kpe, z32, kpe[:, :, swap]], axis=2))
    w_q_b = np.asarray(inp['w_q_b']).reshape(2, 384, 8, 96)
    zq = np.zeros((2, 384, 8, 32), np.float32)
    rope = w_q_b[..., 64:96]
    wq = f(np.concatenate([zq, rope, w_q_b[..., 0:64], zq, rope[..., swap]], axis=3).reshape(2, 384, 8 * 192))
    w_kv_b = np.asarray(inp['w_kv_b']).reshape(2, 256, 8, 128)
    wk = f(np.concatenate([np.zeros((2, 256, 8, 64), np.float32), w_kv_b[..., 0:64]], axis=3).reshape(2, 256, 1024))
    wv = f(w_kv_b[..., 64:128].reshape(2, 256, 512))
    wo = f(inp['w_out_ab'])
    wsT = f(np.asarray(inp['w_spatial']).transpose(0, 3, 1, 2).reshape(2, 128, 8 * 128))
    b_s = np.asarray(inp['b_spatial'])
    bsb = f(np.broadcast_to(b_s.reshape(2, 4, 2, 1, 128), (2, 4, 2, 64, 128)).transpose(0, 2, 3, 1, 4).reshape(2, 128, 512))
    vgb = f(np.broadcast_to(np.asarray(inp['gmlp_v_g'])[:, None, :], (2, 128, 512)))
    inv = (10000.0 ** (-np.arange(0, 16, 2, dtype=np.float32) / 16)).astype(np.float32)
    shared = dict(condT=condT, ng1=ng1, ng2=ng2, fg=fg, psc=psc, qag=qag, kvg=kvg, w_in=w_in, wq=wq, wk=wk, wv=wv,
                  wo=wo, wsT=wsT, bsb=bsb, vgb=vgb, w_ff1=f(inp['w_ff1']), w_ff2=f(inp['w_ff2']),
                  w_pool=f(inp['w_pool']))
    cache_ckv = np.asarray(inp['cache_ckv']); cache_kpe = np.asarray(inp['cache_kpe'])
    maps = []
    for r in range(8):
        b, half = r // 2, r % 2
        xp = x_prompt[4 * r:4 * r + 4].reshape(1024, 1024)
        xs = x_sample[b, half * 2048:(half + 1) * 2048]
        m = dict(shared)
        m['xT'] = f(np.concatenate([xp.T, xs.T], axis=1))
        s = np.zeros((128, 4), np.float32); s[:, b] = 1.0
        m['sel'] = s
        wm = w_mod.reshape(4, 1024, 6, 8, 128)[:, :, :, r, :]
        m['wmodr'] = f(wm.reshape(4, 1024, 768))
        bm = b_mod.reshape(4, 6, 8, 128)[:, :, r, :]
        m['bmodr'] = f(bm.reshape(24, 128).T)
        m['cacheT'] = f(np.concatenate([cache_ckv[b].transpose(0, 2, 1), cache_kpe[b].transpose(0, 2, 1)], axis=1))
        pos = np.arange(half * 2048, (half + 1) * 2048)
        row = (pos // 64).astype(np.float32); col = (pos % 64).astype(np.float32)
        ang = np.stack([row[:, None] * inv[None, :], col[:, None] * inv[None, :]], axis=1)
        cosd = np.cos(ang)[:, d // 16, d % 8]
        sind = np.sin(ang)[:, d // 16, d % 8] * np.where((d % 16) // 8 == 0, -1.0, 1.0)[None, :]
        rp = np.zeros((2, 64, 2048), np.float32)
        rp[0, 32:64] = cosd.T; rp[1, 32:64] = sind.T
        m['ropeT'] = rp
        pin = np.zeros((3, 4, 256), np.float32)
        for gi, w in enumerate((2, 4, 8, 16)):
            tt_ = np.arange(256)
            cntP = np.clip(tt_ + w // 2, 0, 256) - np.clip(tt_ - w // 2, 0, 256)
            pin[0, gi] = 1.0 / cntP
            ta = pos[:256]
            pin[1, gi] = 1.0 / (np.clip(ta + w // 2, 0, 4096) - np.clip(ta - w // 2, 0, 4096))
            tb = pos[-256:]
            pin[2, gi] = 1.0 / (np.clip(tb + w // 2, 0, 4096) - np.clip(tb - w // 2, 0, 4096))
        m['pinv'] = f(np.broadcast_to(pin.reshape(1, -1), (128, 3 * 4 * 256)))
        hm = np.zeros((128, 2), np.float32); hm[:, 0] = float(half == 1); hm[:, 1] = float(half == 0)
        m['hmask'] = hm
        maps.append(m)
    return maps


def assemble(results):
    y_prompt = np.zeros((32, 256, 1024), np.float32)
    y_sample = np.zeros((4, 4096, 1024), np.float32)
    new_ckv = np.zeros((32, 2, 256, 256), np.float32)
    new_kpe = np.zeros((32, 2, 256, 32), np.float32)
    for r in range(8):
        b, half = r // 2, r % 2
        yT = np.asarray(results[r]['yT'])
        y_prompt[4 * r:4 * r + 4] = yT[:, :1024].T.reshape(4, 256, 1024)
        y_sample[b, half * 2048:(half + 1) * 2048] = yT[:, 1024:].T
        ck = np.asarray(results[r]['ckv_o'])
        kp = np.asarray(results[r]['kpe_o'])
        new_ckv[4 * r:4 * r + 4] = ck.transpose(2, 0, 1).reshape(4, 256, 2, 256).transpose(0, 2, 1, 3)
        new_kpe[4 * r:4 * r + 4] = kp.transpose(2, 0, 1).reshape(4, 256, 2, 32).transpose(0, 2, 1, 3)
    return y_prompt, y_sample, new_ckv, new_kpe


def kernel(**inputs):
    if 'nc' not in _CACHE:
        _CACHE['nc'] = build_nc(set(ALL_STAGES))[0]
    maps = prep_inputs(inputs)
    res = run_bass_kernel_spmd(_CACHE['nc'], maps, core_ids=list(range(8)))
    return assemble(res.results)
```
